# Optimizing a Trainium2 kernel written in Bass

```python
import math
import jax
import jax.numpy as jnp
from jax import lax
import numpy as np

D_MODEL = 2048
BATCH = 1
SEQ = 16384
DEPTH = 2
DEC_BATCH = 8
DEC_SEQ = 4096
PAST_LEN = 128

HEAD_DIM = 128
N_HEADS = D_MODEL // HEAD_DIM
H_NA = N_HEADS // 2
H_DIL = N_HEADS - H_NA
W_NA = H_NA * HEAD_DIM
W_DIL = H_DIL * HEAD_DIM
MIX_W = W_NA + W_DIL
GRID_W = 64
NA_ROWS = 8
NA_COLS = 16
DIL_PAIRS = ((128, 1), (512, 4), (2048, 16))
QB = 128
T5_BUCKETS = 32
T5_MAX_DIST = 2048
D_FF = int(math.ceil(8 * D_MODEL / 3 / 256)) * 256
EPS = 1e-6
NEG = -1e30
SCALE = 1.0 / math.sqrt(HEAD_DIM)

kernel_name = 'hymba_natten_longnet_encoder'


def _rmsnorm(x, g):
    xf = x.astype(jnp.float32)
    y = xf * lax.rsqrt(jnp.mean(xf * xf, axis=-1, keepdims=True) + EPS)
    return (y * g.astype(jnp.float32)).astype(x.dtype)


def _t5_bucket(rel):
    half = T5_BUCKETS // 2
    max_exact = half // 2
    n = np.abs(rel)
    large = max_exact + (np.log(np.maximum(n, max_exact) / max_exact)
                         / np.log(T5_MAX_DIST / max_exact) * (half - max_exact)).astype(np.int32)
    large = np.minimum(large, half - 1)
    return (rel > 0).astype(np.int32) * half + np.where(n < max_exact, n, large).astype(np.int32)


def _band_attention(q, k, v, bias, radius):
    N, L, H, Dh = q.shape
    nb = -(-L // QB)
    pad = nb * QB - L
    kw = QB + 2 * radius
    qp = jnp.pad(q, ((0, 0), (0, pad), (0, 0), (0, 0))).reshape(N, nb, QB, H, Dh)
    kp = jnp.pad(k, ((0, 0), (radius, pad + radius), (0, 0), (0, 0)))
    vp = jnp.pad(v, ((0, 0), (radius, pad + radius), (0, 0), (0, 0)))
    idx = np.arange(nb)[:, None] * QB + np.arange(kw)[None, :]
    ks = kp[:, idx]
    vs = vp[:, idx]
    s = jnp.einsum('nbqhd,nbkhd->nbhqk', qp, ks, preferred_element_type=jnp.float32) * SCALE
    off = np.arange(kw)[None, :] - radius - np.arange(QB)[:, None]
    keypos = idx - radius
    valid = (np.abs(off) <= radius)[None] & ((keypos >= 0) & (keypos < L))[:, None, :]
    b = bias[np.clip(off + radius, 0, 2 * radius)].astype(jnp.float32)
    s = jnp.where(valid[None, :, None], s + jnp.transpose(b, (2, 0, 1))[None, None], NEG)
    m = jnp.max(s, axis=-1, keepdims=True)
    p = jnp.exp(s - m)
    l = jnp.sum(p, axis=-1)
    o = jnp.einsum('nbhqk,nbkhd->nbqhd', p, vs.astype(jnp.float32))
    o = o / jnp.transpose(l, (0, 1, 3, 2))[..., None]
    lse = jnp.transpose(m[..., 0] + jnp.log(l), (0, 1, 3, 2))
    o = o.reshape(N, nb * QB, H, Dh)[:, :L]
    lse = lse.reshape(N, nb * QB, H)[:, :L]
    return o, lse


def _dilated_attention(q, k, v, t5_table):
    B, S, H, Dh = q.shape
    outs, lses = [], []
    for window, dil in DIL_PAIRS:
        radius = window // (2 * dil)
        bias = t5_table[_t5_bucket(dil * np.arange(-radius, radius + 1))]
        def split(a):
            return jnp.transpose(a.reshape(B, S // dil, dil, H, Dh), (0, 2, 1, 3, 4)).reshape(B * dil, S // dil, H, Dh)
        o, lse = _band_attention(split(q), split(k), split(v), bias, radius)
        outs.append(jnp.transpose(o.reshape(B, dil, S // dil, H, Dh), (0, 2, 1, 3, 4)).reshape(B, S, H, Dh))
        lses.append(jnp.transpose(lse.reshape(B, dil, S // dil, H), (0, 2, 1, 3)).reshape(B, S, H))
    wts = jax.nn.softmax(jnp.stack(lses, 0), axis=0)
    return jnp.einsum('pbsh,pbshd->bshd', wts, jnp.stack(outs, 0))


def _neighbourhood_attention(q, k, v, rpb):
    B, L, H, Dh = q.shape
    rows = L // GRID_W
    kh = min(NA_ROWS, rows)
    r = np.arange(rows)
    rs = np.clip(r - kh // 2, 0, rows - kh)
    key_rows = rs[:, None] + np.arange(kh)[None, :]
    c = np.arange(GRID_W)
    cs = np.clip(c - NA_COLS // 2, 0, GRID_W - NA_COLS)
    col_ok = (c[None, :] >= cs[:, None]) & (c[None, :] < cs[:, None] + NA_COLS)
    qg = q.reshape(B, rows, GRID_W, H, Dh)
    kg = k.reshape(B, rows, GRID_W, H, Dh)[:, key_rows]
    vg = v.reshape(B, rows, GRID_W, H, Dh)[:, key_rows]
    s = jnp.einsum('brqhd,brikhd->brhqik', qg, kg, preferred_element_type=jnp.float32) * SCALE
    dr_idx = (key_rows - r[:, None]) + NA_ROWS - 1
    dc_idx = np.clip(c[None, :] - c[:, None] + NA_COLS - 1, 0, 2 * NA_COLS - 2)
    bias = rpb[dr_idx[:, None, :, None], dc_idx[None, :, None, :]]
    bias = jnp.transpose(bias, (0, 4, 1, 2, 3)).astype(jnp.float32)
    s = jnp.where(col_ok[:, None, :], s + bias[None], NEG)
    p = jax.nn.softmax(s.reshape(B, rows, H, GRID_W, kh * GRID_W), axis=-1).reshape(s.shape)
    o = jnp.einsum('brhqik,brikhd->brqhd', p, vg.astype(jnp.float32))
    return o.reshape(B, L, H, Dh)


def _layer(x, w_in, w_out, g_attn, g_na, g_dil, rpb, t5_table, g_ffn, w_gate, w_up, w_down):
    B, L, _ = x.shape
    h = _rmsnorm(x, g_attn)
    proj = h @ w_in
    qa, ka, va, qb, kb, vb = jnp.split(
        proj, [W_NA, 2 * W_NA, 3 * W_NA, 3 * W_NA + W_DIL, 3 * W_NA + 2 * W_DIL], axis=-1)
    heads_a = lambda a: a.reshape(B, L, H_NA, HEAD_DIM)
    heads_b = lambda a: a.reshape(B, L, H_DIL, HEAD_DIM)
    oa = _neighbourhood_attention(heads_a(qa), heads_a(ka), heads_a(va), rpb).reshape(B, L, W_NA).astype(x.dtype)
    ob = _dilated_attention(heads_b(qb), heads_b(kb), heads_b(vb), t5_table).reshape(B, L, W_DIL).astype(x.dtype)
    mix = jnp.concatenate([_rmsnorm(oa, g_na), _rmsnorm(ob, g_dil)], axis=-1)
    x = x + mix @ w_out
    h = _rmsnorm(x, g_ffn)
    return x + (jax.nn.silu(h @ w_gate) * (h @ w_up)) @ w_down


def _trunk(x, w_in, w_out, g_attn, g_na, g_dil, rpb_na, t5_table, g_ffn, w_gate, w_up, w_down, g_final):
    for l in range(DEPTH):
        x = _layer(x, w_in[l], w_out[l], g_attn[l], g_na[l], g_dil[l], rpb_na[l], t5_table,
                   g_ffn[l], w_gate[l], w_up[l], w_down[l])
    return _rmsnorm(x, g_final)


def setup_inputs(seed: int = 0) -> dict:
    key = jax.random.key(seed)
    ks = jax.random.split(key, 16)
    f32 = jnp.float32
    def nrm(k, shape, scale):
        return jax.random.normal(k, shape, f32) * scale
    return {
        'x_prompt': nrm(ks[0], (BATCH, SEQ, D_MODEL), 1.0),
        'x_sample': nrm(ks[1], (DEC_BATCH, DEC_SEQ, D_MODEL), 1.0),
        'w_in': nrm(ks[2], (DEPTH, D_MODEL, 3 * MIX_W), D_MODEL ** -0.5),
        'w_out': nrm(ks[3], (DEPTH, MIX_W, D_MODEL), MIX_W ** -0.5),
        'g_attn': 1.0 + nrm(ks[4], (DEPTH, D_MODEL), 0.02),
        'g_na': 1.0 + nrm(ks[5], (DEPTH, W_NA), 0.02),
        'g_dil': 1.0 + nrm(ks[6], (DEPTH, W_DIL), 0.02),
        'rpb_na': nrm(ks[7], (DEPTH, 2 * NA_ROWS - 1, 2 * NA_COLS - 1, H_NA), 0.1),
        't5_table': nrm(ks[8], (T5_BUCKETS, H_DIL), 0.1),
        'g_ffn': 1.0 + nrm(ks[9], (DEPTH, D_MODEL), 0.02),
        'w_gate': nrm(ks[10], (DEPTH, D_MODEL, D_FF), D_MODEL ** -0.5),
        'w_up': nrm(ks[11], (DEPTH, D_MODEL, D_FF), D_MODEL ** -0.5),
        'w_down': nrm(ks[12], (DEPTH, D_FF, D_MODEL), D_FF ** -0.5),
        'g_final': 1.0 + nrm(ks[13], (D_MODEL,), 0.02),
    }


def reference(x_prompt, x_sample, w_in, w_out, g_attn, g_na, g_dil, rpb_na, t5_table, g_ffn, w_gate, w_up, w_down, g_final):
    y_prompt = _trunk(x_prompt, w_in, w_out, g_attn, g_na, g_dil, rpb_na, t5_table, g_ffn, w_gate, w_up, w_down, g_final)
    y_sample = _trunk(x_sample, w_in, w_out, g_attn, g_na, g_dil, rpb_na, t5_table, g_ffn, w_gate, w_up, w_down, g_final)
    return (y_prompt, y_sample)
```

```python
import contextlib
import math
import numpy as np
import concourse.bass as bass
import concourse.mybir as mybir
from concourse.bass_utils import run_bass_kernel_spmd

F32 = mybir.dt.float32
BF16 = mybir.dt.bfloat16
AF = mybir.ActivationFunctionType
ALU = mybir.AluOpType

NCORES = 8
D = 2048
DC = 16
NH = 16
HD = 128
DFF = 5632
FC = 44
DEPTH = 2
T = 512
LS = 4096
LP = 6144
PAD = 1024
VW = 132
GRID_W = 64
SCALE = 1.0 / math.sqrt(HD)
EPS = 1e-6
NEG = -1e30
DILS = (1, 4, 16)
FF_PARTS = [(0, 8), (8, 8), (16, 8), (24, 8), (32, 8), (40, 4)]

DEBUG_STOP = None


def _t5_bucket(rel):
    nb, md = 32, 2048
    half = nb // 2
    max_exact = half // 2
    n = np.abs(rel)
    large = max_exact + (np.log(np.maximum(n, max_exact) / max_exact)
                         / np.log(md / max_exact) * (half - max_exact)).astype(np.int32)
    large = np.minimum(large, half - 1)
    return (rel > 0).astype(np.int32) * half + np.where(n < max_exact, n, large).astype(np.int32)


def _seg_rows(seg, core):
    if seg == "S":
        return 0, LS // GRID_W, LS // GRID_W
    return 32 * core - 32, 256, LP // GRID_W


def _na_included(seg, core, lq, lk):
    roff, rows, _ = _seg_rows(seg, core)
    gq, gk = lq + roff, lk + roff
    if not (0 <= gk < rows):
        return False
    if 0 <= gq < rows:
        rs = min(max(gq - 4, 0), rows - 8)
    else:
        rs = gq - 4
    return rs <= gk < rs + 8


def _na_plan(seg):
    _, _, lrows = _seg_rows(seg, 0)
    slots = {}
    plan = {}
    qrows = range(0, lrows, 2) if seg == "S" else range(16, 80, 2)
    for lq0 in qrows:
        lst = []
        for kt in range(lq0 - 8, lq0 + 10, 2):
            if kt < -8 or kt + 1 >= lrows + 8:
                continue
            pats = []
            anyinc = False
            for c in range(NCORES):
                pat = tuple(_na_included(seg, c, lq0 + b, kt + a) for a in range(2) for b in range(2))
                anyinc = anyinc or any(pat)
                pats.append(pat)
            if not anyinc:
                continue
            key = (kt - lq0, tuple(pats))
            if key not in slots:
                slots[key] = len(slots)
            lst.append((kt, slots[key]))
        plan[lq0] = lst
    return plan, slots


_NA_PLANS = {s: _na_plan(s) for s in ("S", "P")}


def _na_table(seg, core, rpb_l, hg):
    _, slots = _NA_PLANS[seg]
    ns = len(slots)
    c = np.arange(GRID_W)
    cs = np.clip(c - 8, 0, GRID_W - 16)
    col_ok = (c[None, :] >= cs[:, None]) & (c[None, :] < cs[:, None] + 16)
    dc_idx = np.clip(c[None, :] - c[:, None] + 15, 0, 30)
    out = np.full((128, ns, 2, 128), NEG, np.float32)
    for (dr0, pats), s in slots.items():
        pat = pats[core]
        for a in range(2):
            for b in range(2):
                if not pat[a * 2 + b]:
                    continue
                dr = dr0 + a - b
                if abs(dr) > 7:
                    continue
                for hh in range(2):
                    vals = rpb_l[dr + 7][:, 2 * hg + hh][dc_idx]
                    blk = np.where(col_ok, vals, np.float32(NEG))
                    out[a * 64:(a + 1) * 64, s, hh, b * 64:(b + 1) * 64] = blk.T
    return out


def _dil_table(t5, hd):
    out = np.full((128, 3, 2, 2, 128), NEG, np.float32)
    i = np.arange(128)[:, None]
    j = np.arange(128)[None, :]
    for bi, dil in enumerate(DILS):
        for kt in range(2):
            off = (i - 64 + 128 * kt) - j
            ok = np.abs(off) <= 64
            bidx = _t5_bucket(dil * np.clip(off, -64, 64))
            for hh in range(2):
                vals = t5[:, 2 * hd + hh][bidx]
                out[:, bi, kt, hh, :] = np.where(ok, vals, np.float32(NEG))
    return out


def _km_layout(L):
    base = []
    n = 0
    for dil in DILS:
        cnt = L // (64 * dil) + 1
        base.append((n, cnt))
        n += dil * cnt
    return base, n


def _km_table(seg, core):
    L = LS if seg == "S" else LP
    base, ncol = _km_layout(L)
    out = np.zeros((128, ncol), np.float32)
    goff, gl = (0, LS) if seg == "S" else (2048 * core - 2048, 16384)
    for bi, dil in enumerate(DILS):
        b0, cnt = base[bi]
        for r in range(dil):
            for ti in range(cnt):
                tok = (64 * ti - 64 + np.arange(128)) * dil + r
                ok = (tok >= 0) & (tok < L) & (tok + goff >= 0) & (tok + goff < gl)
                out[:, b0 + r * cnt + ti] = np.where(ok, np.float32(0.0), np.float32(NEG))
    return out


class Tok:
    __slots__ = ("sem", "val")

    def __init__(self, sem, val):
        self.sem = sem
        self.val = val


class Eng:
    def __init__(self, kb, name, handle, selfwait):
        self.name = name
        self.h = handle
        self.sem = kb.new_sem("e_" + name)
        self.cnt = 0
        self.waited = {}
        self.selfwait = selfwait

    def wait(self, tok):
        if tok is None:
            return
        if tok.sem is self.sem and not self.selfwait:
            return
        k = tok.sem.num
        if self.waited.get(k, 0) >= tok.val:
            return
        self.h.wait_ge(tok.sem, tok.val)
        self.waited[k] = tok.val

    def finish(self, ins):
        ins.then_inc(self.sem, 1)
        self.cnt += 1
        return Tok(self.sem, self.cnt)


class Buf:
    def __init__(self, kb, name, t=None):
        self.kb = kb
        self.name = name
        self.t = t
        self.w = None
        self.r = {}
        self.dsem = None
        self.dcnt = 0
        kb.bufs.append(self)

    def dma_sem(self):
        if self.dsem is None:
            if self.kb.free_dsems:
                self.dsem, self.dcnt = self.kb.free_dsems.pop()
            else:
                self.dsem = self.kb.new_sem("d%d" % self.kb.nsem)
        return self.dsem


class KB:
    def __init__(self, nc, es):
        self.nc = nc
        self.es = es
        self.bufs = []
        self.nsem = 0
        self.free_dsems = []
        self.pe = Eng(self, "pe", nc.tensor, False)
        self.act = Eng(self, "act", nc.scalar, True)
        self.dve = Eng(self, "dve", nc.vector, True)
        self.pool = Eng(self, "pool", nc.gpsimd, True)
        self.sp = Eng(self, "sp", nc.sync, False)
        self.engs = [self.pe, self.act, self.dve, self.pool, self.sp]

    def new_sem(self, name):
        self.nsem += 1
        return self.es.enter_context(self.nc.semaphore(name))

    def sb(self, name, shape, dt, es=None):
        self.nt = getattr(self, "nt", 0) + 1
        name = "%s_%d" % (name, self.nt)
        t = (es or self.es).enter_context(self.nc.sbuf_tensor(name, shape, dt))
        return Buf(self, name, t)

    def ps(self, name, shape, dt, es=None):
        self.nt = getattr(self, "nt", 0) + 1
        name = "%s_%d" % (name, self.nt)
        t = (es or self.es).enter_context(self.nc.psum_tensor(name, shape, dt))
        return Buf(self, name, t)

    def op(self, eng, fn, reads=(), writes=()):
        for b in reads:
            eng.wait(b.w)
        for b in writes:
            eng.wait(b.w)
            for t in b.r.values():
                eng.wait(t)
        ins = fn()
        tok = eng.finish(ins)
        for b in reads:
            b.r[eng.name] = tok
        for b in writes:
            b.w = tok
            b.r = {}
        return tok

    def dma(self, q, out_ap, in_ap, owner, reads=(), writes=()):
        for b in reads:
            q.wait(b.w)
        sem = owner.dma_sem()
        for b in writes:
            if not (b.w is not None and b.w.sem is sem):
                q.wait(b.w)
            for t in b.r.values():
                q.wait(t)
        q.h.dma_start(out=out_ap, in_=in_ap).then_inc(sem, 16)
        owner.dcnt += 16
        tok = Tok(sem, owner.dcnt)
        for b in reads:
            b.r["dma_" + owner.name] = tok
        for b in writes:
            b.w = tok
            b.r = {}
        return tok

    def barrier(self, only=None):
        if only is None:
            toks = [Tok(e.sem, e.cnt) for e in self.engs if e.cnt > 0]
            toks += [Tok(b.dsem, b.dcnt) for b in self.bufs if b.dsem is not None and b.dcnt > 0]
        else:
            toks = [Tok(b.dsem, b.dcnt) for b in only if b.dsem is not None and b.dcnt > 0]
        for e in self.engs:
            for t in toks:
                if t.sem is e.sem:
                    continue
                e.wait(t)

    def retire(self, mark):
        for b in self.bufs[mark:]:
            if b.dsem is not None:
                self.free_dsems.append((b.dsem, b.dcnt))
        del self.bufs[mark:]


def bcast_last(ap, n):
    return bass.AP(ap.tensor, ap.offset, [list(x) for x in ap.ap] + [[0, n]])


def build_program(debug_stop=None, debug_out=()):
    nc = bass.Bass("TRN2", target_bir_lowering=False)
    es = contextlib.ExitStack()
    DEBUG_STOP = debug_stop
    dram = lambda n, s, dt, kind: nc.dram_tensor(n, list(s), dt, kind=("ExternalOutput" if n in debug_out else kind))
    xs = dram("xs", [LS, D], F32, "ExternalInput")
    xp = dram("xp", [LP, D], F32, "ExternalInput")
    w_in = dram("w_in", [DEPTH, D, 3 * D], F32, "ExternalInput")
    w_out = dram("w_out", [DEPTH, D, D], F32, "ExternalInput")
    w_gate = dram("w_gate", [DEPTH, D, DFF], F32, "ExternalInput")
    w_up = dram("w_up", [DEPTH, D, DFF], F32, "ExternalInput")
    w_down = dram("w_down", [DEPTH, DFF, D], F32, "ExternalInput")
    g_attn = dram("g_attn", [DEPTH, D], F32, "ExternalInput")
    g_ffn = dram("g_ffn", [DEPTH, D], F32, "ExternalInput")
    g_mix = dram("g_mix", [DEPTH, D], F32, "ExternalInput")
    g_final = dram("g_final", [1, D], F32, "ExternalInput")
    identd = dram("identd", [128, 128], F32, "ExternalInput")
    zerosd = dram("zerosd", [128, NH * VW], F32, "ExternalInput")
    nsl = {s: len(_NA_PLANS[s][1]) for s in ("S", "P")}
    nab_d = {s: dram("nab" + s, [DEPTH, 4, 128, nsl[s] * 256], F32, "ExternalInput") for s in ("S", "P")}
    dlb_d = dram("dlb", [4, 128, 3 * 2 * 2 * 128], F32, "ExternalInput")
    kml = {"S": _km_layout(LS), "P": _km_layout(LP)}
    km_d = {s: dram("km" + s, [128, kml[s][1]], F32, "ExternalInput") for s in ("S", "P")}
    ys = dram("ys", [LS, D], F32, "ExternalOutput")
    yp = dram("yp", [2048, D], F32, "ExternalOutput")
    wb = {
        "in": dram("wb_in", [DEPTH, D, 3 * D], BF16, "Internal"),
        "out": dram("wb_out", [DEPTH, D, D], BF16, "Internal"),
        "gate": dram("wb_gate", [DEPTH, D, DFF], BF16, "Internal"),
        "up": dram("wb_up", [DEPTH, D, DFF], BF16, "Internal"),
        "down": dram("wb_down", [DEPTH, DFF, D], BF16, "Internal"),
    }
    wsrc = {"in": w_in, "out": w_out, "gate": w_gate, "up": w_up, "down": w_down}
    SEG = {}
    for s, L, xin in (("S", LS, xs), ("P", LP, xp)):
        SEG[s] = dict(
            name=s, L=L, xin=xin,
            QT=dram("QT" + s, [NH, HD, L + 2 * PAD], BF16, "Internal"),
            KT=dram("KT" + s, [NH, HD, L + 2 * PAD], BF16, "Internal"),
            V=dram("V" + s, [L + 2 * PAD, NH * VW], BF16, "Internal"),
            ONA=dram("ONA" + s, [L, 8 * HD], F32, "Internal"),
            UD=[dram("UD%d%s" % (b, s), [L, 8 * VW], F32, "Internal") for b in range(3)],
            X1=dram("X1" + s, [L, D], F32, "Internal"),
        )
    RANGE_A0 = {"S": (0, LS), "P": (0, LP)}
    RANGE_L = [{"S": (0, LS), "P": (1024, 5120)}, {"S": (0, LS), "P": (2048, 4096)}]

    with es:
        kb = KB(nc, es)
        pe, act, dve, pool, sp = kb.pe, kb.act, kb.dve, kb.pool, kb.sp
        ident = kb.sb("ident", [128, 128], BF16)
        epsb = kb.sb("epsb", [128, 1], F32)
        stats = [[kb.sb("stats%d_%d" % (i, j), [128, 16], F32) for j in range(4)] for i in range(4)]
        kmt = {s: kb.sb("kmt" + s, [128, kml[s][1]], F32) for s in ("S", "P")}
        tp_banks = []
        ps_banks = []
        es.enter_context(nc.Block())

        kb.dma(pool, ident.t[:], identd.ap(), ident, writes=[ident])
        kb.op(dve, lambda: nc.vector.memset(epsb.t[:], EPS), writes=[epsb])
        for s in ("S", "P"):
            kb.dma(sp, kmt[s].t[:], km_d[s].ap(), kmt[s], writes=[kmt[s]])
        wready = {}
        order = [("in", 0), ("out", 0), ("gate", 0), ("up", 0), ("down", 0),
                 ("in", 1), ("out", 1), ("gate", 1), ("up", 1), ("down", 1)]
        n_early = 8 + 8 + 8 + 22 + 8
        cast_q = []
        for nm, l in order:
            b = Buf(kb, "wc_%s%d" % (nm, l))
            wready[(nm, l)] = b
            rows = wsrc[nm].ap().shape[1]
            step = 256
            for r0 in range(0, rows, step):
                cast_q.append((nm, l, r0, step, b))

        cast_q = [c for c in cast_q if not (c[0] == "in" and c[1] == 0)]
        win0 = []
        for j in range(12):
            b = Buf(kb, "wc_in0_%d" % j)
            win0.append(b)
            kb.dma(pool, wb["in"].ap()[0, :, j * 512:(j + 1) * 512], wsrc["in"].ap()[0, :, j * 512:(j + 1) * 512], b,
                   writes=[b])
        n_total = len(cast_q)

        def emit_casts(n=None, layer=None):
            k = 0
            while cast_q and (n is None or k < n) and (layer is None or len(cast_q) > n_total - n_early):
                nm, l, r0, step, b = cast_q.pop(0)
                kb.dma(pool, wb[nm].ap()[l, r0:r0 + step, :], wsrc[nm].ap()[l, r0:r0 + step, :], b, writes=[b])
                k += 1

        def emit_zero_fill():
            zb = Buf(kb, "zfill")
            sgS = SEG["S"]
            for side in (0, PAD + LS):
                src = bass.AP(zerosd, 0, [[0, NH], [NH * VW, 128], [1, PAD]])
                kb.dma(pool, sgS["KT"].ap()[:, :, side:side + PAD], src, zb, writes=[zb])
                for n in range(PAD // 128):
                    kb.dma(pool, sgS["V"].ap()[side + n * 128: side + (n + 1) * 128, :], zerosd.ap(), zb, writes=[zb])

        evac_rr = [0]

        def evac_engine():
            evac_rr[0] += 1
            return act if (evac_rr[0] & 1) else dve

        def copy_on(eng, out_ap, in_ap, reads, writes, scale=None):
            if eng is act:
                if scale is None:
                    return kb.op(act, lambda: nc.scalar.copy(out=out_ap, in_=in_ap), reads, writes)
                return kb.op(act, lambda: nc.scalar.mul(out=out_ap, in_=in_ap, mul=scale), reads, writes)
            if scale is None:
                return kb.op(dve, lambda: nc.vector.tensor_copy(out=out_ap, in_=in_ap), reads, writes)
            return kb.op(dve, lambda: nc.vector.tensor_scalar_mul(out=out_ap, in0=in_ap, scalar1=scale), reads, writes)

        bank_rr = [0]

        def next_bank():
            b = ps_banks[bank_rr[0] % 6]
            bank_rr[0] += 1
            return b

        def rms_stats(src_ap, junk_buf, junk_ap, src_bufs, st, col, nfeat):
            kb.op(act, lambda: nc.scalar.activation(out=junk_ap, in_=src_ap, func=AF.Square,
                                                    accum_out=st.t[:, col:col + 1]),
                  reads=src_bufs, writes=[junk_buf, st])
            kb.op(act, lambda: nc.scalar.activation(out=st.t[:, col:col + 1], in_=st.t[:, col:col + 1], func=AF.Sqrt,
                                                    scale=1.0 / nfeat, bias=epsb.t[:, 0:1]),
                  reads=[epsb], writes=[st])
            kb.op(dve, lambda: nc.vector.reciprocal(out=st.t[:, col + 1:col + 2], in_=st.t[:, col:col + 1]),
                  writes=[st])

        def transposes(src, s, dstT):
            for half in range(2):
                tpb = tp_banks[half]
                tpv = tpb.t[:]

                def mk(half=half, tpv=tpv):
                    ins = None
                    for k in range(8):
                        kc = half * 8 + k
                        ins = nc.tensor.transpose(out=tpv[:, k * 128:(k + 1) * 128],
                                                  in_=src.t[:, kc * 128:(kc + 1) * 128], identity=ident.t[:])
                    return ins
                kb.op(pe, mk, reads=[src, ident], writes=[tpb])
                copy_on(evac_engine(), dstT.t[:, half * 8:(half + 1) * 8, s * 128:(s + 1) * 128],
                        tpv.rearrange("p (a b) -> p a b", a=8), [tpb], [dstT])

        def dense_phase(layer_prev, do_c, do_a, a_layer, last):
            mark = len(kb.bufs)
            with contextlib.ExitStack() as esd:
                tp_banks[:] = [kb.ps("tpb%d" % i, [128, 1024], BF16, esd) for i in range(2)]
                ps_banks[:] = [kb.ps("psb%d" % i, [128, 512], F32, esd) for i in range(6)]
                xt = kb.sb("xt", [128, 4, D], F32, esd)
                xts = [Buf(kb, "xts%d" % i, xt.t) for i in range(4)]

                def x_load(src, t0):
                    for s_ in range(4):
                        kb.dma(sp, xt.t[:, s_, :], src.ap()[t0 + s_ * 128:t0 + (s_ + 1) * 128, :], xts[s_], writes=[xts[s_]])

                def x_store(dst, t0):
                    for s_ in range(4):
                        kb.dma(pool, dst.ap()[t0 + s_ * 128:t0 + (s_ + 1) * 128, :], xt.t[:, s_, :], xts[s_], reads=[xts[s_]])
                hTs = [kb.sb("hT%d" % i, [128, DC, T], BF16, esd) for i in range(2)]
                hpre = [kb.sb("hpre%d" % i, [128, D], BF16, esd) for i in range(2)]
                wsl = [kb.sb("wsl%d" % i, [128, DC, 512], BF16, esd) for i in range(3)]
                qkst = [kb.sb("qkst%d" % i, [128, 4, T], BF16, esd) for i in range(2)]
                vst = [kb.sb("vst%d" % i, [128, 4, VW], BF16, esd) for i in range(4)]
                gA = kb.sb("gA", [128, D], F32, esd)
                if do_c:
                    actT = [kb.sb("actT%d" % i, [128, 8, T], BF16, esd) for i in range(2)]
                    sgt = [kb.sb("sgt%d" % i, [128, T], F32, esd) for i in range(2)]
                    onab = [kb.sb("onab%d" % i, [128, 8 * HD], F32, esd) for i in range(2)]
                    udb = [kb.sb("udb%d" % i, [128, 8 * VW], F32, esd) for i in range(3)]
                    gF = kb.sb("gF", [128, D], F32, esd)
                    gM = kb.sb("gM", [128, D], F32, esd)
                    ost = [kb.sb("ost%d" % i, [128, 1024], F32, esd) for i in range(2)]
                wrr = [0]

                def wslot():
                    b = wsl[wrr[0] % 3]
                    wrr[0] += 1
                    return b

                for v in vst:
                    kb.op(dve, lambda v=v: nc.vector.memset(v.t[:], 0.0), writes=[v])
                    kb.op(dve, lambda v=v: nc.vector.memset(v.t[:, :, 128:129], 1.0), writes=[v])
                if do_a:
                    kb.dma(sp, gA.t[:], g_attn.ap()[a_layer:a_layer + 1, :].partition_broadcast(128), gA, writes=[gA])
                elif last:
                    kb.dma(sp, gA.t[:], g_final.ap().partition_broadcast(128), gA, writes=[gA])
                if do_c:
                    kb.dma(sp, gF.t[:], g_ffn.ap()[layer_prev:layer_prev + 1, :].partition_broadcast(128), gF, writes=[gF])
                    kb.dma(sp, gM.t[:], g_mix.ap()[layer_prev:layer_prev + 1, :].partition_broadcast(128), gM, writes=[gM])

                def wload(slot, nm, l, rows, r0, c0, ncols, dst_c0=0, nk=DC):
                    src = wb[nm].ap()[l, r0:r0 + nk * 128, c0:c0 + ncols].rearrange("(kc p) n -> p kc n", p=128)
                    dep = win0[c0 // 512] if (nm == "in" and l == 0) else wready[(nm, l)]
                    kb.dma(sp, slot.t[:, 0:nk, dst_c0:dst_c0 + ncols], src, slot, reads=[dep], writes=[slot])

                def norm_events(gt, dstT, sts):
                    def prep(s):
                        hp = hpre[s % 2]
                        st = sts[s]
                        rms_stats(xt.t[:, s, :], hp, hp.t[:], [xts[s]], st, 0, D)
                        kb.op(dve, lambda: nc.vector.scalar_tensor_tensor(
                            out=hp.t[:], in0=xt.t[:, s, :], scalar=st.t[:, 1:2], in1=gt.t[:],
                            op0=ALU.mult, op1=ALU.mult), reads=[xts[s], st, gt], writes=[hp])

                    def tr(s):
                        transposes(hpre[s % 2], s, dstT)
                    return [lambda: (prep(0), prep(1)),
                            lambda: (tr(0), prep(2)),
                            lambda: (tr(1), prep(3)),
                            lambda: tr(2),
                            lambda: tr(3)]

                def norm_T(gt, dstT, sts):
                    for ev in norm_events(gt, dstT, sts):
                        ev()

                def qkv(l, sg, t0, hT, kv_only=False, events=(), cast_spread=False):
                    jl = [j for j in range(12) if not (kv_only and (j // 2) % 3 == 0)]
                    events = list(events)
                    ne = len(events)
                    at = {int(round((e + 1) * len(jl) / (ne + 1))): e for e in range(ne)}
                    assert len(at) == ne
                    for ji, j in enumerate(jl):
                        if ji in at:
                            events[at[ji]]()
                        if cast_spread and ji % 4 == 1:
                            emit_casts(1, layer=0)
                        slot = wslot()
                        wload(slot, "in", l, D, 0, j * 512, 512)
                        kind = ("q", "k", "v")[(j // 2) % 3]
                        h0 = (j % 2) * 4 + (8 if j >= 6 else 0)
                        if kind in ("q", "k"):
                            st_ = qkst[j % 2]
                            for hh in range(4):
                                bank = next_bank()

                                def mk(hh=hh, bank=bank, slot=slot):
                                    ins = None
                                    for kc in range(DC):
                                        ins = nc.tensor.matmul(bank.t[:], slot.t[:, kc, hh * 128:(hh + 1) * 128],
                                                               hT.t[:, kc, :], start=(kc == 0), stop=(kc == DC - 1))
                                    return ins
                                kb.op(pe, mk, reads=[slot, hT], writes=[bank])
                                copy_on(evac_engine(), st_.t[:, hh, :], bank.t[:], [bank], [st_],
                                        scale=(SCALE if kind == "q" else None))
                            dst = sg["QT" if kind == "q" else "KT"].ap()[h0:h0 + 4, :, PAD + t0:PAD + t0 + T]
                            kb.dma(pool, dst.rearrange("h d t -> d h t"), st_.t[:], st_, reads=[st_])
                        else:
                            for s in range(4):
                                bank = next_bank()

                                def mk(s=s, bank=bank, slot=slot):
                                    ins = None
                                    for kc in range(DC):
                                        ins = nc.tensor.matmul(bank.t[:], hT.t[:, kc, s * 128:(s + 1) * 128],
                                                               slot.t[:, kc, :], start=(kc == 0), stop=(kc == DC - 1))
                                    return ins
                                kb.op(pe, mk, reads=[slot, hT], writes=[bank])
                                v = vst[s]
                                copy_on(evac_engine(), v.t[:, :, 0:128], bank.t[:].rearrange("p (h d) -> p h d", h=4),
                                        [bank], [v])
                                r0 = PAD + t0 + s * 128
                                kb.dma(pool, sg["V"].ap()[r0:r0 + 128, h0 * VW:(h0 + 4) * VW],
                                       v.t[:].rearrange("p h c -> p (h c)"), v, reads=[v])

                def xadd(bank, s, q):
                    kb.op(dve, lambda: nc.vector.tensor_tensor(out=xt.t[:, s, q * 512:(q + 1) * 512], in0=bank.t[:],
                                                               in1=xt.t[:, s, q * 512:(q + 1) * 512], op=ALU.add),
                          reads=[bank], writes=[xts[s]])

                def mix_loads(sg, tok0, i):
                    kb.dma(sp, onab[i % 2].t[:], sg["ONA"].ap()[tok0:tok0 + 128, :], onab[i % 2], writes=[onab[i % 2]])
                    for b in range(3):
                        kb.dma(sp, udb[b].t[:], sg["UD"][b].ap()[tok0:tok0 + 128, :], udb[b], writes=[udb[b]])

                def mix_stage(sg, t0, s, mixT):
                    st = stats[1][s]
                    ona = onab[s % 2]
                    mix_loads(sg, t0 + s * 128, s)
                    kb.op(pool, lambda: nc.gpsimd.tensor_tensor(out=udb[0].t[:], in0=udb[0].t[:], in1=udb[1].t[:],
                                                                op=ALU.add), reads=[udb[1]], writes=[udb[0]])
                    kb.op(pool, lambda: nc.gpsimd.tensor_tensor(out=udb[0].t[:], in0=udb[0].t[:], in1=udb[2].t[:],
                                                                op=ALU.add), reads=[udb[2]], writes=[udb[0]])
                    u0v = udb[0].t[:].rearrange("p (h c) -> p h c", h=8)
                    rl = st.t[:, 8:16]
                    kb.op(dve, lambda: nc.vector.tensor_scalar_add(out=rl, in0=u0v[:, :, 128], scalar1=1e-30),
                          reads=[udb[0]], writes=[st])
                    kb.op(dve, lambda: nc.vector.reciprocal(out=rl, in_=rl), writes=[st])
                    odil = udb[1].t[:, 0:1024]
                    kb.op(dve, lambda: nc.vector.tensor_tensor(out=odil.rearrange("p (h d) -> p h d", h=8),
                                                               in0=u0v[:, :, 0:128], in1=bcast_last(rl, 128),
                                                               op=ALU.mult), reads=[udb[0], st], writes=[udb[1]])
                    hp = hpre[s % 2]
                    rms_stats(ona.t[:], hp, hp.t[:, 0:1024], [ona], st, 0, 1024)
                    rms_stats(odil, hp, hp.t[:, 1024:2048], [udb[1]], st, 2, 1024)
                    kb.op(dve, lambda: nc.vector.scalar_tensor_tensor(
                        out=hp.t[:, 0:1024], in0=ona.t[:], scalar=st.t[:, 1:2], in1=gM.t[:, 0:1024],
                        op0=ALU.mult, op1=ALU.mult), reads=[ona, st, gM], writes=[hp])
                    kb.op(dve, lambda: nc.vector.scalar_tensor_tensor(
                        out=hp.t[:, 1024:2048], in0=odil, scalar=st.t[:, 3:4], in1=gM.t[:, 1024:2048],
                        op0=ALU.mult, op1=ALU.mult), reads=[udb[1], st, gM], writes=[hp])


                def mix_T(s, mixT):
                    transposes(hpre[s % 2], s, mixT)

                def mix_events(nxt):
                    mt = hTs[0]
                    return [lambda: mix_stage(nxt[0], nxt[1], 0, mt),
                            lambda: (mix_T(0, mt), mix_stage(nxt[0], nxt[1], 1, mt)),
                            lambda: (mix_T(1, mt), mix_stage(nxt[0], nxt[1], 2, mt)),
                            lambda: (mix_T(2, mt), mix_stage(nxt[0], nxt[1], 3, mt)),
                            lambda: mix_T(3, mt)]

                def c_tile(l, sg, t0, ffn_events=()):
                    xsrc = sg["xin"] if l == 0 else sg["X1"]
                    mixT, hTf = hTs[0], hTs[1]
                    for q in range(4):
                        slot = wslot()
                        wload(slot, "out", l, D, 0, q * 512, 512)
                        if q == 0 and not xpref[0]:
                            x_load(xsrc, t0)
                        xpref[0] = False if q == 0 else xpref[0]
                        for s in range(4):
                            bank = next_bank()

                            def mk(s=s, bank=bank, slot=slot):
                                ins = None
                                for kc in range(DC):
                                    ins = nc.tensor.matmul(bank.t[:], mixT.t[:, kc, s * 128:(s + 1) * 128],
                                                           slot.t[:, kc, :], start=(kc == 0), stop=(kc == DC - 1))
                                return ins
                            kb.op(pe, mk, reads=[slot, mixT], writes=[bank])
                            xadd(bank, s, q)
                    norm_T(gF, hTf, stats[2])
                    ffn_events = list(ffn_events)
                    for pi, (j0, nj) in enumerate(FF_PARTS):
                        if pi >= 1 and pi - 1 < len(ffn_events):
                            ffn_events[pi - 1]()
                        emit_casts(1)
                        at = actT[pi % 2]
                        for jp in range(0, nj, 2):
                            slot = wslot()
                            wload(slot, "gate", l, D, 0, (j0 + jp) * 128, 256, 0)
                            wload(slot, "up", l, D, 0, (j0 + jp) * 128, 256, 256)
                            for c in range(2):
                                bg, bu = next_bank(), next_bank()

                                def mk(c=c, bg=bg, bu=bu, slot=slot):
                                    ins = None
                                    for kc in range(DC):
                                        ins = nc.tensor.matmul(bg.t[:], slot.t[:, kc, c * 128:(c + 1) * 128],
                                                               hTf.t[:, kc, :], start=(kc == 0), stop=(kc == DC - 1))
                                    for kc in range(DC):
                                        ins = nc.tensor.matmul(bu.t[:], slot.t[:, kc, 256 + c * 128:256 + (c + 1) * 128],
                                                               hTf.t[:, kc, :], start=(kc == 0), stop=(kc == DC - 1))
                                    return ins
                                kb.op(pe, mk, reads=[slot, hTf], writes=[bg, bu])
                                sg_ = sgt[(jp + c) % 2]
                                kb.op(act, lambda bg=bg, sg_=sg_: nc.scalar.activation(out=sg_.t[:], in_=bg.t[:], func=AF.Silu),
                                      reads=[bg], writes=[sg_])
                                kb.op(dve, lambda bu=bu, sg_=sg_, jj=jp + c, at=at: nc.vector.tensor_tensor(
                                    out=at.t[:, jj, :], in0=bu.t[:], in1=sg_.t[:], op=ALU.mult),
                                    reads=[bu, sg_], writes=[at])
                        for q in range(4):
                            slot = wslot()
                            wload(slot, "down", l, DFF, j0 * 128, q * 512, 512, 0, nk=nj)
                            for s in range(4):
                                bank = next_bank()

                                def mk(s=s, bank=bank, slot=slot, at=at, nj=nj):
                                    ins = None
                                    for jj in range(nj):
                                        ins = nc.tensor.matmul(bank.t[:], at.t[:, jj, s * 128:(s + 1) * 128],
                                                               slot.t[:, jj, :], start=(jj == 0), stop=(jj == nj - 1))
                                    return ins
                                kb.op(pe, mk, reads=[slot, at], writes=[bank])
                                xadd(bank, s, q)

                fin_cnt = [0]
                fin_ring = []
                if do_c:
                    fin_ring = [(b_, b_.t[:, 0:1024]) for b_ in (ost + onab + udb)]

                def final_tile(sg, t0):
                    for s in range(4):
                        hp = actT[0]
                        st = stats[3][s]
                        rms_stats(xt.t[:, s, :], hp, hp.t[:, 0:4, :].rearrange("p a b -> p (a b)"), [xts[s]], st, 0, D)
                        for hf in range(2):
                            o, ov = fin_ring[fin_cnt[0] % len(fin_ring)]
                            fin_cnt[0] += 1
                            kb.op(dve, lambda s=s, hf=hf, o=o, st=st: nc.vector.scalar_tensor_tensor(
                                out=ov, in0=xt.t[:, s, hf * 1024:(hf + 1) * 1024], scalar=st.t[:, 1:2],
                                in1=gA.t[:, hf * 1024:(hf + 1) * 1024], op0=ALU.mult, op1=ALU.mult),
                                reads=[xts[s], st, gA], writes=[o])
                            if sg["name"] == "S":
                                dst = ys.ap()[t0 + s * 128:t0 + (s + 1) * 128, hf * 1024:(hf + 1) * 1024]
                            else:
                                r0 = t0 - 2048 + s * 128
                                dst = yp.ap()[r0:r0 + 128, hf * 1024:(hf + 1) * 1024]
                            kb.dma(pool, dst, ov, o, reads=[o])

                xpref = [False]
                tiles = []
                for s in ("S", "P"):
                    sg = SEG[s]
                    lo, hi = RANGE_L[layer_prev][s] if do_c else RANGE_A0[s]
                    for t0 in range(lo, hi, T):
                        kvo = False
                        if s == "P" and do_a:
                            qlo, qhi = RANGE_L[a_layer][s]
                            kvo = not (qlo <= t0 < qhi)
                        tiles.append((sg, t0, kvo))
                if do_c:
                    for ev in mix_events(tiles[0]):
                        ev()
                for ti, (sg, t0, kvo) in enumerate(tiles):
                    nxt = tiles[ti + 1] if ti + 1 < len(tiles) else None
                    if do_c:
                        c_tile(layer_prev, sg, t0, ffn_events=(mix_events(nxt) if (last and nxt is not None) else ()))
                        if last:
                            final_tile(sg, t0)
                        else:
                            x_store(sg["X1"], t0)
                    else:
                        if not xpref[0]:
                            x_load(sg["xin"], t0)
                        xpref[0] = False
                    if do_a:
                        hTa = hTs[1] if do_c else hTs[ti % 2]
                        if do_c or ti == 0:
                            norm_T(gA, hTa, stats[0])
                        evs = mix_events(nxt) if (do_c and nxt is not None) else []
                        if nxt is not None:
                            if do_c:
                                nsrc = nxt[0]["xin"] if layer_prev == 0 else nxt[0]["X1"]
                            else:
                                nsrc = nxt[0]["xin"]

                            def xev(nsrc=nsrc, nxt=nxt):
                                x_load(nsrc, nxt[1])
                                xpref[0] = True
                            evs = [xev] + evs
                            if not do_c:
                                evs = evs + norm_events(gA, hTs[(ti + 1) % 2], stats[0])
                        qkv(a_layer, sg, t0, hTa, kv_only=kvo, events=evs, cast_spread=(not do_c))
                        if (not do_c) and ti == 0:
                            emit_zero_fill()
                kb.barrier()
            kb.retire(mark)

        def attn_phase(l):
            mark = len(kb.bufs)
            with contextlib.ExitStack() as esa:
                tp_banks[:] = []
                ps_banks[:] = [kb.ps("psa%d" % i, [128, 512], F32, esa) for i in range(8)]
                qts = [kb.sb("qts%d" % i, [128, 2, 2048], BF16, esa) for i in range(2)]
                kts = [kb.sb("kts%d" % i, [128, 2, 4096], BF16, esa) for i in range(2)]
                vbuf = [kb.sb("vbuf%d" % i, [128, 32, 2 * VW], BF16, esa) for i in range(2)]
                nsm = max(nsl.values())
                nab = [kb.sb("nab%d" % i, [128, nsm, 2, 128], BF16, esa) for i in range(2)]
                dlb = [kb.sb("dlbt%d" % i, [128, 3, 2, 2, 128], BF16, esa) for i in range(2)]
                pts = [kb.sb("pt%d" % i, [128, 256], BF16, esa) for i in range(4)]
                osts = [kb.sb("osta%d" % i, [128, 2, HD], F32, esa) for i in range(3)]
                usts = [kb.sb("usta%d" % i, [128, 2, VW], F32, esa) for i in range(3)]
                rls = [kb.sb("rls%d" % i, [128, 2], F32, esa) for i in range(4)]
                for u in usts:
                    kb.op(dve, lambda u=u: nc.vector.memset(u.t[:], 0.0), writes=[u])
                sc_slots = [(ps_banks[i], 0) for i in range(4)]
                u_banks = ps_banks[4:8]
                cnt = dict(sc=0, pt=0, ub=0, job=0, st=0, v=0, unit=0, rl=0)
                pending = []
                LAG = 3

                def flush(n_keep):
                    while len(pending) > n_keep:
                        pending.pop(0)()

                def tile_pair2(k_aps, q_aps, bias_ap, km_ap, km_buf, v_aps, ubs, first, lastt, rbufs, vb, fin):
                    scb, c0 = sc_slots[cnt["sc"] % 4]
                    cnt["sc"] += 1
                    ptb = pts[cnt["pt"] % 4]
                    cnt["pt"] += 1

                    def mk():
                        ins = None
                        for hh in range(2):
                            nc.tensor.matmul(scb.t[:, c0 + hh * 128:c0 + (hh + 1) * 128], k_aps[hh], q_aps[hh],
                                             start=True, stop=False)
                            ins = nc.tensor.matmul(scb.t[:, c0 + hh * 128:c0 + (hh + 1) * 128], ident.t[:],
                                                   bias_ap[:, hh * 128:(hh + 1) * 128], start=False, stop=True)
                        return ins
                    kb.op(pe, mk, reads=rbufs + [ident], writes=[scb])

                    def stage2():
                        if km_ap is None:
                            kb.op(act, lambda: nc.scalar.activation(out=ptb.t[:], in_=scb.t[:, c0:c0 + 256], func=AF.Exp),
                                  reads=[scb], writes=[ptb])
                        else:
                            kb.op(act, lambda: nc.scalar.activation(out=ptb.t[:], in_=scb.t[:, c0:c0 + 256], func=AF.Exp,
                                                                    bias=km_ap, scale=1.0),
                                  reads=[scb, km_buf], writes=[ptb])
                        for hh in range(2):
                            kb.op(pe, lambda hh=hh: nc.tensor.matmul(ubs[hh].t[:, 0:HD + 1], ptb.t[:, hh * 128:(hh + 1) * 128],
                                                                     v_aps[hh], start=first, stop=lastt),
                                  reads=[ptb, vb], writes=[ubs[hh]])
                        if lastt:
                            fin()
                    pending.append(stage2)
                    flush(LAG)

                def na_fin(ubs, ost, sg, tk, h0):
                    def fin():
                        for hh in range(2):
                            ub = ubs[hh]
                            rl = rls[cnt["rl"] % 4]
                            cnt["rl"] += 1
                            kb.op(dve, lambda: nc.vector.tensor_scalar_add(out=rl.t[:, 0:1], in0=ub.t[:, HD:HD + 1], scalar1=1e-30),
                                  reads=[ub], writes=[rl])
                            kb.op(dve, lambda: nc.vector.reciprocal(out=rl.t[:, 1:2], in_=rl.t[:, 0:1]), writes=[rl])
                            kb.op(dve, lambda: nc.vector.tensor_scalar_mul(out=ost.t[:, hh, :], in0=ub.t[:, 0:HD], scalar1=rl.t[:, 1:2]),
                                  reads=[ub, rl], writes=[ost])
                        kb.dma(pool, sg["ONA"].ap()[tk:tk + 128, h0 * HD:(h0 + 2) * HD],
                               ost.t[:].rearrange("p h d -> p (h d)"), ost, reads=[ost])
                    return fin

                def dil_fin(ubs, ust, sg, bi, tb, dil, hd):
                    def fin():
                        for hh in range(2):
                            ub = ubs[hh]
                            eng = act if hh == 0 else dve
                            copy_on(eng, ust.t[:, hh, 0:HD + 1], ub.t[:, 0:HD + 1], [ub], [ust])
                        dst = sg["UD"][bi].ap()[tb:tb + 127 * dil + 1:dil, hd * 2 * VW:(hd + 1) * 2 * VW]
                        kb.dma(pool, dst, ust.t[:].rearrange("p h c -> p (h c)"), ust, reads=[ust])
                    return fin

                for s in ("S", "P"):
                    sg = SEG[s]
                    lo, hi = RANGE_L[l][s]
                    plan, _ = _NA_PLANS[s]
                    kbase, _ = kml[s]
                    for hg in range(8):
                        h0 = 2 * hg
                        ui = cnt["unit"]
                        cnt["unit"] += 1

                        def table_load(s_, hg_, ui_):
                            if hg_ < 4:
                                nb_ = nab[ui_ % 2]
                                kb.dma(pool, nb_.t[:, 0:nsl[s_], :, :].rearrange("p a b c -> p (a b c)"),
                                       nab_d[s_].ap()[l, hg_], nb_, writes=[nb_])
                            else:
                                db_ = dlb[ui_ % 2]
                                kb.dma(pool, db_.t[:].rearrange("p a b c d -> p (a b c d)"), dlb_d.ap()[hg_ - 4], db_,
                                       writes=[db_])
                        if ui == 0:
                            table_load(s, hg, ui)
                        nxt_unit = (s, hg + 1) if hg < 7 else (("P", 0) if s == "S" else None)
                        if nxt_unit is not None:
                            table_load(nxt_unit[0], nxt_unit[1], ui + 1)
                        if hg < 4:
                            nb = nab[ui % 2]
                        else:
                            db = dlb[ui % 2]
                        for q0 in range(lo, hi, 2048):
                            qt = qts[cnt["st"] % 2]
                            kt = kts[cnt["st"] % 2]
                            cnt["st"] += 1
                            kb.dma(sp, qt.t[:], sg["QT"].ap()[h0:h0 + 2, :, PAD + q0:PAD + q0 + 2048].rearrange("h d t -> d h t"),
                                   qt, writes=[qt])
                            kb.dma(sp, kt.t[:], sg["KT"].ap()[h0:h0 + 2, :, q0:q0 + 4096].rearrange("h d t -> d h t"),
                                   kt, writes=[kt])
                            if hg < 4:
                                vb = vbuf[cnt["v"] % 2]
                                cnt["v"] += 1
                                r0 = PAD + q0 - 512
                                kb.dma(sp, vb.t[:, 0:24, :],
                                       sg["V"].ap()[r0:r0 + 24 * 128, h0 * VW:(h0 + 2) * VW].rearrange("(n p) c -> p n c", p=128),
                                       vb, writes=[vb])
                                for b in range(16):
                                    lq0 = q0 // GRID_W + 2 * b
                                    pairs = plan[lq0]
                                    ost = osts[cnt["job"] % 3]
                                    cnt["job"] += 1
                                    ubs = [u_banks[(cnt["ub"] + hh) % 4] for hh in range(2)]
                                    cnt["ub"] += 2
                                    fin = na_fin(ubs, ost, sg, q0 + 128 * b, h0)
                                    for pi, (ktr, slot) in enumerate(pairs):
                                        ktok = ktr * GRID_W
                                        koff = ktok - (q0 - 1024)
                                        vn = (ktok - (q0 - 512)) // 128
                                        assert 0 <= vn < 24 and 0 <= koff <= 4096 - 128
                                        tile_pair2([kt.t[:, hh, koff:koff + 128] for hh in range(2)],
                                                   [qt.t[:, hh, 128 * b:128 * b + 128] for hh in range(2)],
                                                   nb.t[:, slot, :, :].rearrange("p h q -> p (h q)"), None, None,
                                                   [vb.t[:, vn, hh * VW:hh * VW + HD + 1] for hh in range(2)],
                                                   ubs, pi == 0, pi == len(pairs) - 1, [kt, qt, nb], vb, fin)
                            else:
                                hd = hg - 4
                                for bi, dil in enumerate(DILS):
                                    nblk = 2048 // (128 * dil)
                                    ntile = nblk + 1
                                    vb = vbuf[cnt["v"] % 2]
                                    cnt["v"] += 1
                                    mA = q0 // dil - 64
                                    for r in range(dil):
                                        base = PAD + mA * dil + r
                                        src = sg["V"].ap()[base:base + dil * (128 * ntile - 1) + 1:dil, h0 * VW:(h0 + 2) * VW]
                                        kb.dma(sp, vb.t[:, r * ntile:(r + 1) * ntile, :],
                                               src.rearrange("(n p) c -> p n c", p=128), vb, writes=[vb])
                                    kb0, kcnt = kbase[bi]
                                    for r in range(dil):
                                        for blk in range(nblk):
                                            ust = usts[cnt["job"] % 3]
                                            cnt["job"] += 1
                                            qc0 = blk * 128 * dil + r
                                            ubs = [u_banks[(cnt["ub"] + hh) % 4] for hh in range(2)]
                                            cnt["ub"] += 2
                                            fin = dil_fin(ubs, ust, sg, bi, q0 + blk * 128 * dil + r, dil, hd)
                                            for ktile in range(2):
                                                kc0 = (-64 + 128 * (blk + ktile)) * dil + r + 1024
                                                m_t = mA + 128 * (blk + ktile)
                                                assert (m_t + 64) % 64 == 0 and 0 <= (m_t + 64) // 64 < kcnt
                                                kmcol = kb0 + r * kcnt + (m_t + 64) // 64
                                                assert 0 <= kc0 and kc0 + 127 * dil < 4096
                                                tile_pair2([kt.t[:, hh, kc0:kc0 + 127 * dil + 1:dil] for hh in range(2)],
                                                           [qt.t[:, hh, qc0:qc0 + 127 * dil + 1:dil] for hh in range(2)],
                                                           db.t[:, bi, ktile, :, :].rearrange("p h q -> p (h q)"),
                                                           kmt[s].t[:, kmcol:kmcol + 1], kmt[s],
                                                           [vb.t[:, r * ntile + blk + ktile, hh * VW:hh * VW + HD + 1] for hh in range(2)],
                                                           ubs, ktile == 0, ktile == 1, [kt, qt, db], vb, fin)
                    flush(0)
                flush(0)
                kb.barrier()
            kb.retire(mark)

        dense_phase(None, False, True, 0, False)
        emit_casts(None, layer=0)
        if DEBUG_STOP != "A0":
            attn_phase(0)
            if DEBUG_STOP != "B0":
                dense_phase(0, True, True, 1, False)
                if DEBUG_STOP != "C0":
                    attn_phase(1)
                    dense_phase(1, True, False, None, True)
        kb.barrier()
    return nc


_NC_CACHE = {}


def kernel(x_prompt, x_sample, w_in, w_out, g_attn, g_na, g_dil, rpb_na, t5_table, g_ffn, w_gate, w_up, w_down,
           g_final):
    f = lambda a: np.ascontiguousarray(np.asarray(a, dtype=np.float32))
    x_prompt, x_sample = f(x_prompt), f(x_sample)
    w_in, w_out, w_gate, w_up, w_down = f(w_in), f(w_out), f(w_gate), f(w_up), f(w_down)
    g_attn, g_ffn, g_final = f(g_attn), f(g_ffn), f(g_final)
    rpb_na, t5_table = f(rpb_na), f(t5_table)
    g_mix = np.ascontiguousarray(np.concatenate([f(g_na), f(g_dil)], axis=1))
    if "nc" not in _NC_CACHE:
        _NC_CACHE["nc"] = build_program()
    nc = _NC_CACHE["nc"]
    ident = np.eye(128, dtype=np.float32)
    dlb = np.stack([_dil_table(t5_table, hd).reshape(128, -1) for hd in range(4)])
    xpad = np.zeros((2048 + 16384 + 2048, D), np.float32)
    xpad[2048:2048 + 16384] = x_prompt[0]
    nabS = np.stack([np.stack([_na_table("S", 0, rpb_na[l], hg).reshape(128, -1) for hg in range(4)])
                     for l in range(DEPTH)])
    kmS = _km_table("S", 0)
    in_maps = []
    for c in range(NCORES):
        nabP = np.stack([np.stack([_na_table("P", c, rpb_na[l], hg).reshape(128, -1) for hg in range(4)])
                         for l in range(DEPTH)])
        in_maps.append({
            "xs": x_sample[c], "xp": np.ascontiguousarray(xpad[2048 * c:2048 * c + LP]),
            "w_in": w_in, "w_out": w_out, "w_gate": w_gate, "w_up": w_up, "w_down": w_down,
            "g_attn": g_attn, "g_ffn": g_ffn, "g_mix": g_mix, "g_final": g_final.reshape(1, D),
            "identd": ident, "zerosd": np.zeros((128, NH * VW), np.float32), "nabS": nabS, "nabP": nabP, "dlb": dlb, "kmS": kmS, "kmP": _km_table("P", c),
        })
    res = run_bass_kernel_spmd(nc, in_maps, core_ids=list(range(NCORES)))
    y_sample = np.stack([np.asarray(res.results[c]["ys"], dtype=np.float32) for c in range(NCORES)])
    y_prompt = np.concatenate([np.asarray(res.results[c]["yp"], dtype=np.float32) for c in range(NCORES)])[None]
    return (y_prompt, y_sample)
```

```python
import contextlib
import math
import numpy as np
import concourse.bass as bass
import concourse.mybir as mybir
from concourse.bass_utils import run_bass_kernel_spmd

F32 = mybir.dt.float32
BF16 = mybir.dt.bfloat16
AF = mybir.ActivationFunctionType
ALU = mybir.AluOpType

NCORES = 8
D = 2048
DC = 16
NH = 16
HD = 128
DFF = 5632
FC = 44
DEPTH = 2
T = 512
LS = 4096
LP = 6144
PAD = 1024
VW = 132
GRID_W = 64
SCALE = 1.0 / math.sqrt(HD)
EPS = 1e-6
NEG = -1e30
DILS = (1, 4, 16)
FF_PARTS = [(0, 8), (8, 8), (16, 8), (24, 8), (32, 8), (40, 4)]

DEBUG_STOP = None


def _t5_bucket(rel):
    nb, md = 32, 2048
    half = nb // 2
    max_exact = half // 2
    n = np.abs(rel)
    large = max_exact + (np.log(np.maximum(n, max_exact) / max_exact)
                         / np.log(md / max_exact) * (half - max_exact)).astype(np.int32)
    large = np.minimum(large, half - 1)
    return (rel > 0).astype(np.int32) * half + np.where(n < max_exact, n, large).astype(np.int32)


def _seg_rows(seg, core):
    if seg == "S":
        return 0, LS // GRID_W, LS // GRID_W
    return 32 * core - 32, 256, LP // GRID_W


def _na_included(seg, core, lq, lk):
    roff, rows, _ = _seg_rows(seg, core)
    gq, gk = lq + roff, lk + roff
    if not (0 <= gk < rows):
        return False
    if 0 <= gq < rows:
        rs = min(max(gq - 4, 0), rows - 8)
    else:
        rs = gq - 4
    return rs <= gk < rs + 8


def _na_plan(seg):
    _, _, lrows = _seg_rows(seg, 0)
    slots = {}
    plan = {}
    qrows = range(0, lrows, 2) if seg == "S" else range(16, 80, 2)
    for lq0 in qrows:
        lst = []
        for kt in range(lq0 - 8, lq0 + 10, 2):
            if kt < -8 or kt + 1 >= lrows + 8:
                continue
            pats = []
            anyinc = False
            for c in range(NCORES):
                pat = tuple(_na_included(seg, c, lq0 + b, kt + a) for a in range(2) for b in range(2))
                anyinc = anyinc or any(pat)
                pats.append(pat)
            if not anyinc:
                continue
            key = (kt - lq0, tuple(pats))
            if key not in slots:
                slots[key] = len(slots)
            lst.append((kt, slots[key]))
        plan[lq0] = lst
    return plan, slots


_NA_PLANS = {s: _na_plan(s) for s in ("S", "P")}


def _na_table(seg, core, rpb_l, hg):
    _, slots = _NA_PLANS[seg]
    ns = len(slots)
    c = np.arange(GRID_W)
    cs = np.clip(c - 8, 0, GRID_W - 16)
    col_ok = (c[None, :] >= cs[:, None]) & (c[None, :] < cs[:, None] + 16)
    dc_idx = np.clip(c[None, :] - c[:, None] + 15, 0, 30)
    out = np.full((128, ns, 2, 128), NEG, np.float32)
    for (dr0, pats), s in slots.items():
        pat = pats[core]
        for a in range(2):
            for b in range(2):
                if not pat[a * 2 + b]:
                    continue
                dr = dr0 + a - b
                if abs(dr) > 7:
                    continue
                for hh in range(2):
                    vals = rpb_l[dr + 7][:, 2 * hg + hh][dc_idx]
                    blk = np.where(col_ok, vals, np.float32(NEG))
                    out[a * 64:(a + 1) * 64, s, hh, b * 64:(b + 1) * 64] = blk.T
    return out


def _dil_table(t5, hd):
    out = np.full((128, 3, 2, 2, 128), NEG, np.float32)
    i = np.arange(128)[:, None]
    j = np.arange(128)[None, :]
    for bi, dil in enumerate(DILS):
        for kt in range(2):
            off = (i - 64 + 128 * kt) - j
            ok = np.abs(off) <= 64
            bidx = _t5_bucket(dil * np.clip(off, -64, 64))
            for hh in range(2):
                vals = t5[:, 2 * hd + hh][bidx]
                out[:, bi, kt, hh, :] = np.where(ok, vals, np.float32(NEG))
    return out


def _km_layout(L):
    base = []
    n = 0
    for dil in DILS:
        cnt = L // (64 * dil) + 1
        base.append((n, cnt))
        n += dil * cnt
    return base, n


def _km_table(seg, core):
    L = LS if seg == "S" else LP
    base, ncol = _km_layout(L)
    out = np.zeros((128, ncol), np.float32)
    goff, gl = (0, LS) if seg == "S" else (2048 * core - 2048, 16384)
    for bi, dil in enumerate(DILS):
        b0, cnt = base[bi]
        for r in range(dil):
            for ti in range(cnt):
                tok = (64 * ti - 64 + np.arange(128)) * dil + r
                ok = (tok >= 0) & (tok < L) & (tok + goff >= 0) & (tok + goff < gl)
                out[:, b0 + r * cnt + ti] = np.where(ok, np.float32(0.0), np.float32(NEG))
    return out


class Tok:
    __slots__ = ("sem", "val")

    def __init__(self, sem, val):
        self.sem = sem
        self.val = val


class Eng:
    def __init__(self, kb, name, handle, selfwait):
        self.name = name
        self.h = handle
        self.sem = kb.new_sem("e_" + name)
        self.cnt = 0
        self.waited = {}
        self.selfwait = selfwait

    def wait(self, tok):
        if tok is None:
            return
        if tok.sem is self.sem and not self.selfwait:
            return
        k = tok.sem.num
        if self.waited.get(k, 0) >= tok.val:
            return
        self.h.wait_ge(tok.sem, tok.val)
        self.waited[k] = tok.val

    def finish(self, ins):
        ins.then_inc(self.sem, 1)
        self.cnt += 1
        return Tok(self.sem, self.cnt)


class Buf:
    def __init__(self, kb, name, t=None):
        self.kb = kb
        self.name = name
        self.t = t
        self.w = None
        self.r = {}
        self.dsem = None
        self.dcnt = 0
        kb.bufs.append(self)

    def dma_sem(self):
        if self.dsem is None:
            if self.kb.free_dsems:
                self.dsem, self.dcnt = self.kb.free_dsems.pop()
            else:
                self.dsem = self.kb.new_sem("d%d" % self.kb.nsem)
        return self.dsem


class KB:
    def __init__(self, nc, es):
        self.nc = nc
        self.es = es
        self.bufs = []
        self.nsem = 0
        self.free_dsems = []
        self.pe = Eng(self, "pe", nc.tensor, False)
        self.act = Eng(self, "act", nc.scalar, True)
        self.dve = Eng(self, "dve", nc.vector, True)
        self.pool = Eng(self, "pool", nc.gpsimd, True)
        self.sp = Eng(self, "sp", nc.sync, False)
        self.engs = [self.pe, self.act, self.dve, self.pool, self.sp]

    def new_sem(self, name):
        self.nsem += 1
        return self.es.enter_context(self.nc.semaphore(name))

    def sb(self, name, shape, dt, es=None):
        self.nt = getattr(self, "nt", 0) + 1
        name = "%s_%d" % (name, self.nt)
        t = (es or self.es).enter_context(self.nc.sbuf_tensor(name, shape, dt))
        return Buf(self, name, t)

    def ps(self, name, shape, dt, es=None):
        self.nt = getattr(self, "nt", 0) + 1
        name = "%s_%d" % (name, self.nt)
        t = (es or self.es).enter_context(self.nc.psum_tensor(name, shape, dt))
        return Buf(self, name, t)

    def op(self, eng, fn, reads=(), writes=()):
        for b in reads:
            eng.wait(b.w)
        for b in writes:
            eng.wait(b.w)
            for t in b.r.values():
                eng.wait(t)
        ins = fn()
        tok = eng.finish(ins)
        for b in reads:
            b.r[eng.name] = tok
        for b in writes:
            b.w = tok
            b.r = {}
        return tok

    def dma(self, q, out_ap, in_ap, owner, reads=(), writes=()):
        for b in reads:
            q.wait(b.w)
        sem = owner.dma_sem()
        for b in writes:
            if not (b.w is not None and b.w.sem is sem):
                q.wait(b.w)
            for t in b.r.values():
                q.wait(t)
        q.h.dma_start(out=out_ap, in_=in_ap).then_inc(sem, 16)
        owner.dcnt += 16
        tok = Tok(sem, owner.dcnt)
        for b in reads:
            b.r["dma_" + owner.name] = tok
        for b in writes:
            b.w = tok
            b.r = {}
        return tok

    def barrier(self, only=None):
        if only is None:
            toks = [Tok(e.sem, e.cnt) for e in self.engs if e.cnt > 0]
            toks += [Tok(b.dsem, b.dcnt) for b in self.bufs if b.dsem is not None and b.dcnt > 0]
        else:
            toks = [Tok(b.dsem, b.dcnt) for b in only if b.dsem is not None and b.dcnt > 0]
        for e in self.engs:
            for t in toks:
                if t.sem is e.sem:
                    continue
                e.wait(t)

    def retire(self, mark):
        for b in self.bufs[mark:]:
            if b.dsem is not None:
                self.free_dsems.append((b.dsem, b.dcnt))
        del self.bufs[mark:]


def bcast_last(ap, n):
    return bass.AP(ap.tensor, ap.offset, [list(x) for x in ap.ap] + [[0, n]])


def build_program(debug_stop=None, debug_out=()):
    nc = bass.Bass("TRN2", target_bir_lowering=False)
    es = contextlib.ExitStack()
    DEBUG_STOP = debug_stop
    dram = lambda n, s, dt, kind: nc.dram_tensor(n, list(s), dt, kind=("ExternalOutput" if n in debug_out else kind))
    xs = dram("xs", [LS, D], F32, "ExternalInput")
    xp = dram("xp", [LP, D], F32, "ExternalInput")
    w_in = dram("w_in", [DEPTH, D, 3 * D], F32, "ExternalInput")
    w_out = dram("w_out", [DEPTH, D, D], F32, "ExternalInput")
    w_gate = dram("w_gate", [DEPTH, D, DFF], F32, "ExternalInput")
    w_up = dram("w_up", [DEPTH, D, DFF], F32, "ExternalInput")
    w_down = dram("w_down", [DEPTH, DFF, D], F32, "ExternalInput")
    g_attn = dram("g_attn", [DEPTH, D], F32, "ExternalInput")
    g_ffn = dram("g_ffn", [DEPTH, D], F32, "ExternalInput")
    g_mix = dram("g_mix", [DEPTH, D], F32, "ExternalInput")
    g_final = dram("g_final", [1, D], F32, "ExternalInput")
    identd = dram("identd", [128, 128], F32, "ExternalInput")
    zerosd = dram("zerosd", [128, NH * VW], F32, "ExternalInput")
    nsl = {s: len(_NA_PLANS[s][1]) for s in ("S", "P")}
    nab_d = {s: dram("nab" + s, [DEPTH, 4, 128, nsl[s] * 256], F32, "ExternalInput") for s in ("S", "P")}
    dlb_d = dram("dlb", [4, 128, 3 * 2 * 2 * 128], F32, "ExternalInput")
    kml = {"S": _km_layout(LS), "P": _km_layout(LP)}
    km_d = {s: dram("km" + s, [128, kml[s][1]], F32, "ExternalInput") for s in ("S", "P")}
    ys = dram("ys", [LS, D], F32, "ExternalOutput")
    yp = dram("yp", [2048, D], F32, "ExternalOutput")
    wb = {
        "in": dram("wb_in", [DEPTH, D, 3 * D], BF16, "Internal"),
        "out": dram("wb_out", [DEPTH, D, D], BF16, "Internal"),
        "gate": dram("wb_gate", [DEPTH, D, DFF], BF16, "Internal"),
        "up": dram("wb_up", [DEPTH, D, DFF], BF16, "Internal"),
        "down": dram("wb_down", [DEPTH, DFF, D], BF16, "Internal"),
    }
    wsrc = {"in": w_in, "out": w_out, "gate": w_gate, "up": w_up, "down": w_down}
    SEG = {}
    for s, L, xin in (("S", LS, xs), ("P", LP, xp)):
        SEG[s] = dict(
            name=s, L=L, xin=xin,
            QT=dram("QT" + s, [NH, HD, L + 2 * PAD], BF16, "Internal"),
            KT=dram("KT" + s, [NH, HD, L + 2 * PAD], BF16, "Internal"),
            V=dram("V" + s, [L + 2 * PAD, NH * VW], BF16, "Internal"),
            ONA=dram("ONA" + s, [L, 8 * HD], F32, "Internal"),
            UD=[dram("UD%d%s" % (b, s), [L, 8 * VW], F32, "Internal") for b in range(3)],
            X1=dram("X1" + s, [L, D], F32, "Internal"),
        )
    RANGE_A0 = {"S": (0, LS), "P": (0, LP)}
    RANGE_L = [{"S": (0, LS), "P": (1024, 5120)}, {"S": (0, LS), "P": (2048, 4096)}]

    with es:
        kb = KB(nc, es)
        pe, act, dve, pool, sp = kb.pe, kb.act, kb.dve, kb.pool, kb.sp
        ident = kb.sb("ident", [128, 128], BF16)
        epsb = kb.sb("epsb", [128, 1], F32)
        stats = [[kb.sb("stats%d_%d" % (i, j), [128, 16], F32) for j in range(4)] for i in range(4)]
        kmt = {s: kb.sb("kmt" + s, [128, kml[s][1]], F32) for s in ("S", "P")}
        tp_banks = []
        ps_banks = []
        es.enter_context(nc.Block())

        kb.dma(pool, ident.t[:], identd.ap(), ident, writes=[ident])
        kb.op(dve, lambda: nc.vector.memset(epsb.t[:], EPS), writes=[epsb])
        for s in ("S", "P"):
            kb.dma(sp, kmt[s].t[:], km_d[s].ap(), kmt[s], writes=[kmt[s]])
        wready = {}
        order = [("in", 0), ("out", 0), ("gate", 0), ("up", 0), ("down", 0),
                 ("in", 1), ("out", 1), ("gate", 1), ("up", 1), ("down", 1)]
        n_early = 8 + 8 + 8 + 22 + 8
        cast_q = []
        for nm, l in order:
            b = Buf(kb, "wc_%s%d" % (nm, l))
            wready[(nm, l)] = b
            rows = wsrc[nm].ap().shape[1]
            step = 256
            for r0 in range(0, rows, step):
                cast_q.append((nm, l, r0, step, b))

        cast_q = [c for c in cast_q if not (c[0] == "in" and c[1] == 0)]
        win0 = []
        for j in range(12):
            b = Buf(kb, "wc_in0_%d" % j)
            win0.append(b)
            kb.dma(pool, wb["in"].ap()[0, :, j * 512:(j + 1) * 512], wsrc["in"].ap()[0, :, j * 512:(j + 1) * 512], b,
                   writes=[b])
        n_total = len(cast_q)

        def emit_casts(n=None, layer=None):
            k = 0
            while cast_q and (n is None or k < n) and (layer is None or len(cast_q) > n_total - n_early):
                nm, l, r0, step, b = cast_q.pop(0)
                kb.dma(pool, wb[nm].ap()[l, r0:r0 + step, :], wsrc[nm].ap()[l, r0:r0 + step, :], b, writes=[b])
                k += 1

        def emit_zero_fill():
            zb = Buf(kb, "zfill")
            sgS = SEG["S"]
            for side in (0, PAD + LS):
                src = bass.AP(zerosd, 0, [[0, NH], [NH * VW, 128], [1, PAD]])
                kb.dma(pool, sgS["KT"].ap()[:, :, side:side + PAD], src, zb, writes=[zb])
                for n in range(PAD // 128):
                    kb.dma(pool, sgS["V"].ap()[side + n * 128: side + (n + 1) * 128, :], zerosd.ap(), zb, writes=[zb])

        evac_rr = [0]

        def evac_engine():
            evac_rr[0] += 1
            return act if (evac_rr[0] & 1) else dve

        def copy_on(eng, out_ap, in_ap, reads, writes, scale=None):
            if eng is act:
                if scale is None:
                    return kb.op(act, lambda: nc.scalar.copy(out=out_ap, in_=in_ap), reads, writes)
                return kb.op(act, lambda: nc.scalar.mul(out=out_ap, in_=in_ap, mul=scale), reads, writes)
            if scale is None:
                return kb.op(dve, lambda: nc.vector.tensor_copy(out=out_ap, in_=in_ap), reads, writes)
            return kb.op(dve, lambda: nc.vector.tensor_scalar_mul(out=out_ap, in0=in_ap, scalar1=scale), reads, writes)

        bank_rr = [0]

        def next_bank():
            b = ps_banks[bank_rr[0] % 6]
            bank_rr[0] += 1
            return b

        def rms_stats(src_ap, junk_buf, junk_ap, src_bufs, st, col, nfeat):
            kb.op(act, lambda: nc.scalar.activation(out=junk_ap, in_=src_ap, func=AF.Square,
                                                    accum_out=st.t[:, col:col + 1]),
                  reads=src_bufs, writes=[junk_buf, st])
            kb.op(act, lambda: nc.scalar.activation(out=st.t[:, col:col + 1], in_=st.t[:, col:col + 1], func=AF.Sqrt,
                                                    scale=1.0 / nfeat, bias=epsb.t[:, 0:1]),
                  reads=[epsb], writes=[st])
            kb.op(dve, lambda: nc.vector.reciprocal(out=st.t[:, col + 1:col + 2], in_=st.t[:, col:col + 1]),
                  writes=[st])

        def transposes(src, s, dstT):
            for half in range(2):
                tpb = tp_banks[half]
                tpv = tpb.t[:]

                def mk(half=half, tpv=tpv):
                    ins = None
                    for k in range(8):
                        kc = half * 8 + k
                        ins = nc.tensor.transpose(out=tpv[:, k * 128:(k + 1) * 128],
                                                  in_=src.t[:, kc * 128:(kc + 1) * 128], identity=ident.t[:])
                    return ins
                kb.op(pe, mk, reads=[src, ident], writes=[tpb])
                copy_on(evac_engine(), dstT.t[:, half * 8:(half + 1) * 8, s * 128:(s + 1) * 128],
                        tpv.rearrange("p (a b) -> p a b", a=8), [tpb], [dstT])

        def dense_phase(layer_prev, do_c, do_a, a_layer, last):
            mark = len(kb.bufs)
            with contextlib.ExitStack() as esd:
                tp_banks[:] = [kb.ps("tpb%d" % i, [128, 1024], BF16, esd) for i in range(2)]
                ps_banks[:] = [kb.ps("psb%d" % i, [128, 512], F32, esd) for i in range(6)]
                xt = kb.sb("xt", [128, 4, D], F32, esd)
                xts = [Buf(kb, "xts%d" % i, xt.t) for i in range(4)]

                def x_load(src, t0):
                    for s_ in range(4):
                        kb.dma(sp, xt.t[:, s_, :], src.ap()[t0 + s_ * 128:t0 + (s_ + 1) * 128, :], xts[s_], writes=[xts[s_]])

                def x_store(dst, t0):
                    for s_ in range(4):
                        kb.dma(pool, dst.ap()[t0 + s_ * 128:t0 + (s_ + 1) * 128, :], xt.t[:, s_, :], xts[s_], reads=[xts[s_]])
                hTs = [kb.sb("hT%d" % i, [128, DC, T], BF16, esd) for i in range(2)]
                hpre = [kb.sb("hpre%d" % i, [128, D], BF16, esd) for i in range(2)]
                wsl = [kb.sb("wsl%d" % i, [128, DC, 512], BF16, esd) for i in range(3)]
                qkst = [kb.sb("qkst%d" % i, [128, 4, T], BF16, esd) for i in range(2)]
                vst = [kb.sb("vst%d" % i, [128, 4, VW], BF16, esd) for i in range(4)]
                gA = kb.sb("gA", [128, D], F32, esd)
                if do_c:
                    actT = [kb.sb("actT%d" % i, [128, 8, T], BF16, esd) for i in range(2)]
                    sgt = [kb.sb("sgt%d" % i, [128, T], F32, esd) for i in range(2)]
                    onab = [kb.sb("onab%d" % i, [128, 8 * HD], F32, esd) for i in range(2)]
                    udb = [kb.sb("udb%d" % i, [128, 8 * VW], F32, esd) for i in range(3)]
                    gF = kb.sb("gF", [128, D], F32, esd)
                    gM = kb.sb("gM", [128, D], F32, esd)
                    ost = [kb.sb("ost%d" % i, [128, 1024], F32, esd) for i in range(2)]
                wrr = [0]

                def wslot():
                    b = wsl[wrr[0] % 3]
                    wrr[0] += 1
                    return b

                for v in vst:
                    kb.op(dve, lambda v=v: nc.vector.memset(v.t[:], 0.0), writes=[v])
                    kb.op(dve, lambda v=v: nc.vector.memset(v.t[:, :, 128:129], 1.0), writes=[v])
                if do_a:
                    kb.dma(sp, gA.t[:], g_attn.ap()[a_layer:a_layer + 1, :].partition_broadcast(128), gA, writes=[gA])
                elif last:
                    kb.dma(sp, gA.t[:], g_final.ap().partition_broadcast(128), gA, writes=[gA])
                if do_c:
                    kb.dma(sp, gF.t[:], g_ffn.ap()[layer_prev:layer_prev + 1, :].partition_broadcast(128), gF, writes=[gF])
                    kb.dma(sp, gM.t[:], g_mix.ap()[layer_prev:layer_prev + 1, :].partition_broadcast(128), gM, writes=[gM])

                def wload(slot, nm, l, rows, r0, c0, ncols, dst_c0=0, nk=DC):
                    src = wb[nm].ap()[l, r0:r0 + nk * 128, c0:c0 + ncols].rearrange("(kc p) n -> p kc n", p=128)
                    dep = win0[c0 // 512] if (nm == "in" and l == 0) else wready[(nm, l)]
                    kb.dma(sp, slot.t[:, 0:nk, dst_c0:dst_c0 + ncols], src, slot, reads=[dep], writes=[slot])

                def norm_events(gt, dstT, sts):
                    def prep(s):
                        hp = hpre[s % 2]
                        st = sts[s]
                        rms_stats(xt.t[:, s, :], hp, hp.t[:], [xts[s]], st, 0, D)
                        kb.op(dve, lambda: nc.vector.scalar_tensor_tensor(
                            out=hp.t[:], in0=xt.t[:, s, :], scalar=st.t[:, 1:2], in1=gt.t[:],
                            op0=ALU.mult, op1=ALU.mult), reads=[xts[s], st, gt], writes=[hp])

                    def tr(s):
                        transposes(hpre[s % 2], s, dstT)
                    return [lambda: (prep(0), prep(1)),
                            lambda: (tr(0), prep(2)),
                            lambda: (tr(1), prep(3)),
                            lambda: tr(2),
                            lambda: tr(3)]

                def norm_T(gt, dstT, sts):
                    for ev in norm_events(gt, dstT, sts):
                        ev()

                def qkv(l, sg, t0, hT, kv_only=False, events=(), cast_spread=False):
                    jl = [j for j in range(12) if not (kv_only and (j // 2) % 3 == 0)]
                    events = list(events)
                    ne = len(events)
                    at = {int(round((e + 1) * len(jl) / (ne + 1))): e for e in range(ne)}
                    assert len(at) == ne
                    for ji, j in enumerate(jl):
                        if ji in at:
                            events[at[ji]]()
                        if cast_spread and ji % 4 == 1:
                            emit_casts(1, layer=0)
                        slot = wslot()
                        wload(slot, "in", l, D, 0, j * 512, 512)
                        kind = ("q", "k", "v")[(j // 2) % 3]
                        h0 = (j % 2) * 4 + (8 if j >= 6 else 0)
                        if kind in ("q", "k"):
                            st_ = qkst[j % 2]
                            for hh in range(4):
                                bank = next_bank()

                                def mk(hh=hh, bank=bank, slot=slot):
                                    ins = None
                                    for kc in range(DC):
                                        ins = nc.tensor.matmul(bank.t[:], slot.t[:, kc, hh * 128:(hh + 1) * 128],
                                                               hT.t[:, kc, :], start=(kc == 0), stop=(kc == DC - 1))
                                    return ins
                                kb.op(pe, mk, reads=[slot, hT], writes=[bank])
                                copy_on(evac_engine(), st_.t[:, hh, :], bank.t[:], [bank], [st_],
                                        scale=(SCALE if kind == "q" else None))
                            dst = sg["QT" if kind == "q" else "KT"].ap()[h0:h0 + 4, :, PAD + t0:PAD + t0 + T]
                            kb.dma(pool, dst.rearrange("h d t -> d h t"), st_.t[:], st_, reads=[st_])
                        else:
                            for s in range(4):
                                bank = next_bank()

                                def mk(s=s, bank=bank, slot=slot):
                                    ins = None
                                    for kc in range(DC):
                                        ins = nc.tensor.matmul(bank.t[:], hT.t[:, kc, s * 128:(s + 1) * 128],
                                                               slot.t[:, kc, :], start=(kc == 0), stop=(kc == DC - 1))
                                    return ins
                                kb.op(pe, mk, reads=[slot, hT], writes=[bank])
                                v = vst[s]
                                copy_on(evac_engine(), v.t[:, :, 0:128], bank.t[:].rearrange("p (h d) -> p h d", h=4),
                                        [bank], [v])
                                r0 = PAD + t0 + s * 128
                                kb.dma(pool, sg["V"].ap()[r0:r0 + 128, h0 * VW:(h0 + 4) * VW],
                                       v.t[:].rearrange("p h c -> p (h c)"), v, reads=[v])

                def xadd(bank, s, q):
                    kb.op(dve, lambda: nc.vector.tensor_tensor(out=xt.t[:, s, q * 512:(q + 1) * 512], in0=bank.t[:],
                                                               in1=xt.t[:, s, q * 512:(q + 1) * 512], op=ALU.add),
                          reads=[bank], writes=[xts[s]])

                def mix_loads(sg, tok0, i):
                    kb.dma(sp, onab[i % 2].t[:], sg["ONA"].ap()[tok0:tok0 + 128, :], onab[i % 2], writes=[onab[i % 2]])
                    for b in range(3):
                        kb.dma(sp, udb[b].t[:], sg["UD"][b].ap()[tok0:tok0 + 128, :], udb[b], writes=[udb[b]])

                def mix_stage(sg, t0, s, mixT):
                    st = stats[1][s]
                    ona = onab[s % 2]
                    mix_loads(sg, t0 + s * 128, s)
                    kb.op(pool, lambda: nc.gpsimd.tensor_tensor(out=udb[0].t[:], in0=udb[0].t[:], in1=udb[1].t[:],
                                                                op=ALU.add), reads=[udb[1]], writes=[udb[0]])
                    kb.op(pool, lambda: nc.gpsimd.tensor_tensor(out=udb[0].t[:], in0=udb[0].t[:], in1=udb[2].t[:],
                                                                op=ALU.add), reads=[udb[2]], writes=[udb[0]])
                    u0v = udb[0].t[:].rearrange("p (h c) -> p h c", h=8)
                    rl = st.t[:, 8:16]
                    kb.op(dve, lambda: nc.vector.tensor_scalar_add(out=rl, in0=u0v[:, :, 128], scalar1=1e-30),
                          reads=[udb[0]], writes=[st])
                    kb.op(dve, lambda: nc.vector.reciprocal(out=rl, in_=rl), writes=[st])
                    odil = udb[1].t[:, 0:1024]
                    kb.op(dve, lambda: nc.vector.tensor_tensor(out=odil.rearrange("p (h d) -> p h d", h=8),
                                                               in0=u0v[:, :, 0:128], in1=bcast_last(rl, 128),
                                                               op=ALU.mult), reads=[udb[0], st], writes=[udb[1]])
                    hp = hpre[s % 2]
                    rms_stats(ona.t[:], hp, hp.t[:, 0:1024], [ona], st, 0, 1024)
                    rms_stats(odil, hp, hp.t[:, 1024:2048], [udb[1]], st, 2, 1024)
                    kb.op(dve, lambda: nc.vector.scalar_tensor_tensor(
                        out=hp.t[:, 0:1024], in0=ona.t[:], scalar=st.t[:, 1:2], in1=gM.t[:, 0:1024],
                        op0=ALU.mult, op1=ALU.mult), reads=[ona, st, gM], writes=[hp])
                    kb.op(dve, lambda: nc.vector.scalar_tensor_tensor(
                        out=hp.t[:, 1024:2048], in0=odil, scalar=st.t[:, 3:4], in1=gM.t[:, 1024:2048],
                        op0=ALU.mult, op1=ALU.mult), reads=[udb[1], st, gM], writes=[hp])


                def mix_T(s, mixT):
                    transposes(hpre[s % 2], s, mixT)

                def mix_events(nxt):
                    mt = hTs[0]
                    return [lambda: mix_stage(nxt[0], nxt[1], 0, mt),
                            lambda: (mix_T(0, mt), mix_stage(nxt[0], nxt[1], 1, mt)),
                            lambda: (mix_T(1, mt), mix_stage(nxt[0], nxt[1], 2, mt)),
                            lambda: (mix_T(2, mt), mix_stage(nxt[0], nxt[1], 3, mt)),
                            lambda: mix_T(3, mt)]

                def c_tile(l, sg, t0, ffn_events=()):
                    xsrc = sg["xin"] if l == 0 else sg["X1"]
                    mixT, hTf = hTs[0], hTs[1]
                    for q in range(4):
                        slot = wslot()
                        wload(slot, "out", l, D, 0, q * 512, 512)
                        if q == 0 and not xpref[0]:
                            x_load(xsrc, t0)
                        xpref[0] = False if q == 0 else xpref[0]
                        for s in range(4):
                            bank = next_bank()

                            def mk(s=s, bank=bank, slot=slot):
                                ins = None
                                for kc in range(DC):
                                    ins = nc.tensor.matmul(bank.t[:], mixT.t[:, kc, s * 128:(s + 1) * 128],
                                                           slot.t[:, kc, :], start=(kc == 0), stop=(kc == DC - 1))
                                return ins
                            kb.op(pe, mk, reads=[slot, mixT], writes=[bank])
                            xadd(bank, s, q)
                    norm_T(gF, hTf, stats[2])
                    ffn_events = list(ffn_events)

                    def gu_pair(pi, jp):
                        j0, nj = FF_PARTS[pi]
                        at = actT[pi % 2]
                        slot = wslot()
                        wload(slot, "gate", l, D, 0, (j0 + jp) * 128, 256, 0)
                        wload(slot, "up", l, D, 0, (j0 + jp) * 128, 256, 256)
                        for c in range(2):
                            bg, bu = next_bank(), next_bank()

                            def mk(c=c, bg=bg, bu=bu, slot=slot):
                                ins = None
                                for kc in range(DC):
                                    ins = nc.tensor.matmul(bg.t[:], slot.t[:, kc, c * 128:(c + 1) * 128],
                                                           hTf.t[:, kc, :], start=(kc == 0), stop=(kc == DC - 1))
                                for kc in range(DC):
                                    ins = nc.tensor.matmul(bu.t[:], slot.t[:, kc, 256 + c * 128:256 + (c + 1) * 128],
                                                           hTf.t[:, kc, :], start=(kc == 0), stop=(kc == DC - 1))
                                return ins
                            kb.op(pe, mk, reads=[slot, hTf], writes=[bg, bu])
                            sg_ = sgt[(jp + c) % 2]
                            kb.op(act, lambda bg=bg, sg_=sg_: nc.scalar.activation(out=sg_.t[:], in_=bg.t[:], func=AF.Silu),
                                  reads=[bg], writes=[sg_])
                            kb.op(dve, lambda bu=bu, sg_=sg_, jj=jp + c, at=at: nc.vector.tensor_tensor(
                                out=at.t[:, jj, :], in0=bu.t[:], in1=sg_.t[:], op=ALU.mult),
                                reads=[bu, sg_], writes=[at])

                    def down(pi):
                        j0, nj = FF_PARTS[pi]
                        at = actT[pi % 2]
                        for q in range(4):
                            slot = wslot()
                            wload(slot, "down", l, DFF, j0 * 128, q * 512, 512, 0, nk=nj)
                            for s in range(4):
                                bank = next_bank()

                                def mk(s=s, bank=bank, slot=slot, at=at, nj=nj):
                                    ins = None
                                    for jj in range(nj):
                                        ins = nc.tensor.matmul(bank.t[:], at.t[:, jj, s * 128:(s + 1) * 128],
                                                               slot.t[:, jj, :], start=(jj == 0), stop=(jj == nj - 1))
                                    return ins
                                kb.op(pe, mk, reads=[slot, at], writes=[bank])
                                xadd(bank, s, q)

                    npart = len(FF_PARTS)
                    for jp in range(0, FF_PARTS[0][1], 2):
                        gu_pair(0, jp)
                    for pi in range(npart):
                        if pi >= 1 and pi - 1 < len(ffn_events):
                            ffn_events[pi - 1]()
                        emit_casts(1)
                        if pi + 1 < npart:
                            gu_pair(pi + 1, 0)
                        down(pi)
                        if pi + 1 < npart:
                            for jp in range(2, FF_PARTS[pi + 1][1], 2):
                                gu_pair(pi + 1, jp)

                fin_cnt = [0]
                fin_ring = []
                if do_c:
                    fin_ring = [(b_, b_.t[:, 0:1024]) for b_ in (ost + onab + udb)]

                def final_tile(sg, t0):
                    for s in range(4):
                        hp = actT[0]
                        st = stats[3][s]
                        rms_stats(xt.t[:, s, :], hp, hp.t[:, 0:4, :].rearrange("p a b -> p (a b)"), [xts[s]], st, 0, D)
                        for hf in range(2):
                            o, ov = fin_ring[fin_cnt[0] % len(fin_ring)]
                            fin_cnt[0] += 1
                            kb.op(dve, lambda s=s, hf=hf, o=o, st=st: nc.vector.scalar_tensor_tensor(
                                out=ov, in0=xt.t[:, s, hf * 1024:(hf + 1) * 1024], scalar=st.t[:, 1:2],
                                in1=gA.t[:, hf * 1024:(hf + 1) * 1024], op0=ALU.mult, op1=ALU.mult),
                                reads=[xts[s], st, gA], writes=[o])
                            if sg["name"] == "S":
                                dst = ys.ap()[t0 + s * 128:t0 + (s + 1) * 128, hf * 1024:(hf + 1) * 1024]
                            else:
                                r0 = t0 - 2048 + s * 128
                                dst = yp.ap()[r0:r0 + 128, hf * 1024:(hf + 1) * 1024]
                            kb.dma(pool, dst, ov, o, reads=[o])

                xpref = [False]
                tiles = []
                for s in ("S", "P"):
                    sg = SEG[s]
                    lo, hi = RANGE_L[layer_prev][s] if do_c else RANGE_A0[s]
                    for t0 in range(lo, hi, T):
                        kvo = False
                        if s == "P" and do_a:
                            qlo, qhi = RANGE_L[a_layer][s]
                            kvo = not (qlo <= t0 < qhi)
                        tiles.append((sg, t0, kvo))
                if do_c:
                    for ev in mix_events(tiles[0]):
                        ev()
                for ti, (sg, t0, kvo) in enumerate(tiles):
                    nxt = tiles[ti + 1] if ti + 1 < len(tiles) else None
                    if do_c:
                        c_tile(layer_prev, sg, t0, ffn_events=(mix_events(nxt) if (last and nxt is not None) else ()))
                        if last:
                            final_tile(sg, t0)
                        else:
                            x_store(sg["X1"], t0)
                    else:
                        if not xpref[0]:
                            x_load(sg["xin"], t0)
                        xpref[0] = False
                    if do_a:
                        hTa = hTs[1] if do_c else hTs[ti % 2]
                        if do_c or ti == 0:
                            norm_T(gA, hTa, stats[0])
                        evs = mix_events(nxt) if (do_c and nxt is not None) else []
                        if nxt is not None:
                            if do_c:
                                nsrc = nxt[0]["xin"] if layer_prev == 0 else nxt[0]["X1"]
                            else:
                                nsrc = nxt[0]["xin"]

                            def xev(nsrc=nsrc, nxt=nxt):
                                x_load(nsrc, nxt[1])
                                xpref[0] = True
                            evs = [xev] + evs
                            if not do_c:
                                evs = evs + norm_events(gA, hTs[(ti + 1) % 2], stats[0])
                        qkv(a_layer, sg, t0, hTa, kv_only=kvo, events=evs, cast_spread=(not do_c))
                        if (not do_c) and ti == 0:
                            emit_zero_fill()
                kb.barrier()
            kb.retire(mark)

        def attn_phase(l):
            mark = len(kb.bufs)
            with contextlib.ExitStack() as esa:
                tp_banks[:] = []
                ps_banks[:] = [kb.ps("psa%d" % i, [128, 512], F32, esa) for i in range(8)]
                qts = [kb.sb("qts%d" % i, [128, 2, 2048], BF16, esa) for i in range(2)]
                kts = [kb.sb("kts%d" % i, [128, 2, 4096], BF16, esa) for i in range(2)]
                vbuf = [kb.sb("vbuf%d" % i, [128, 32, 2 * VW], BF16, esa) for i in range(2)]
                nsm = max(nsl.values())
                nab = [kb.sb("nab%d" % i, [128, nsm, 2, 128], BF16, esa) for i in range(2)]
                dlb = [kb.sb("dlbt%d" % i, [128, 3, 2, 2, 128], BF16, esa) for i in range(2)]
                pts = [kb.sb("pt%d" % i, [128, 256], BF16, esa) for i in range(4)]
                osts = [kb.sb("osta%d" % i, [128, 2, HD], F32, esa) for i in range(3)]
                usts = [kb.sb("usta%d" % i, [128, 2, VW], F32, esa) for i in range(3)]
                rls = [kb.sb("rls%d" % i, [128, 2], F32, esa) for i in range(4)]
                for u in usts:
                    kb.op(dve, lambda u=u: nc.vector.memset(u.t[:], 0.0), writes=[u])
                sc_slots = [(ps_banks[i], 0) for i in range(4)]
                u_banks = ps_banks[4:8]
                cnt = dict(sc=0, pt=0, ub=0, job=0, st=0, v=0, unit=0, rl=0)
                pending = []
                LAG = 3

                def flush(n_keep):
                    while len(pending) > n_keep:
                        pending.pop(0)()

                def tile_pair2(k_aps, q_aps, bias_ap, km_ap, km_buf, v_aps, ubs, first, lastt, rbufs, vb, fin):
                    scb, c0 = sc_slots[cnt["sc"] % 4]
                    cnt["sc"] += 1
                    ptb = pts[cnt["pt"] % 4]
                    cnt["pt"] += 1

                    def mk():
                        ins = None
                        for hh in range(2):
                            nc.tensor.matmul(scb.t[:, c0 + hh * 128:c0 + (hh + 1) * 128], k_aps[hh], q_aps[hh],
                                             start=True, stop=False)
                            ins = nc.tensor.matmul(scb.t[:, c0 + hh * 128:c0 + (hh + 1) * 128], ident.t[:],
                                                   bias_ap[:, hh * 128:(hh + 1) * 128], start=False, stop=True)
                        return ins
                    kb.op(pe, mk, reads=rbufs + [ident], writes=[scb])

                    def stage2():
                        if km_ap is None:
                            kb.op(act, lambda: nc.scalar.activation(out=ptb.t[:], in_=scb.t[:, c0:c0 + 256], func=AF.Exp),
                                  reads=[scb], writes=[ptb])
                        else:
                            kb.op(act, lambda: nc.scalar.activation(out=ptb.t[:], in_=scb.t[:, c0:c0 + 256], func=AF.Exp,
                                                                    bias=km_ap, scale=1.0),
                                  reads=[scb, km_buf], writes=[ptb])
                        for hh in range(2):
                            kb.op(pe, lambda hh=hh: nc.tensor.matmul(ubs[hh].t[:, 0:HD + 1], ptb.t[:, hh * 128:(hh + 1) * 128],
                                                                     v_aps[hh], start=first, stop=lastt),
                                  reads=[ptb, vb], writes=[ubs[hh]])
                        if lastt:
                            fin()
                    pending.append(stage2)
                    flush(LAG)

                def na_fin(ubs, ost, sg, tk, h0):
                    def fin():
                        for hh in range(2):
                            ub = ubs[hh]
                            rl = rls[cnt["rl"] % 4]
                            cnt["rl"] += 1
                            kb.op(dve, lambda: nc.vector.tensor_scalar_add(out=rl.t[:, 0:1], in0=ub.t[:, HD:HD + 1], scalar1=1e-30),
                                  reads=[ub], writes=[rl])
                            kb.op(dve, lambda: nc.vector.reciprocal(out=rl.t[:, 1:2], in_=rl.t[:, 0:1]), writes=[rl])
                            kb.op(dve, lambda: nc.vector.tensor_scalar_mul(out=ost.t[:, hh, :], in0=ub.t[:, 0:HD], scalar1=rl.t[:, 1:2]),
                                  reads=[ub, rl], writes=[ost])
                        kb.dma(pool, sg["ONA"].ap()[tk:tk + 128, h0 * HD:(h0 + 2) * HD],
                               ost.t[:].rearrange("p h d -> p (h d)"), ost, reads=[ost])
                    return fin

                def dil_fin(ubs, ust, sg, bi, tb, dil, hd):
                    def fin():
                        for hh in range(2):
                            ub = ubs[hh]
                            eng = act if hh == 0 else dve
                            copy_on(eng, ust.t[:, hh, 0:HD + 1], ub.t[:, 0:HD + 1], [ub], [ust])
                        dst = sg["UD"][bi].ap()[tb:tb + 127 * dil + 1:dil, hd * 2 * VW:(hd + 1) * 2 * VW]
                        kb.dma(pool, dst, ust.t[:].rearrange("p h c -> p (h c)"), ust, reads=[ust])
                    return fin

                for s in ("S", "P"):
                    sg = SEG[s]
                    lo, hi = RANGE_L[l][s]
                    plan, _ = _NA_PLANS[s]
                    kbase, _ = kml[s]
                    for hg in range(8):
                        h0 = 2 * hg
                        ui = cnt["unit"]
                        cnt["unit"] += 1

                        def table_load(s_, hg_, ui_):
                            if hg_ < 4:
                                nb_ = nab[ui_ % 2]
                                kb.dma(pool, nb_.t[:, 0:nsl[s_], :, :].rearrange("p a b c -> p (a b c)"),
                                       nab_d[s_].ap()[l, hg_], nb_, writes=[nb_])
                            else:
                                db_ = dlb[ui_ % 2]
                                kb.dma(pool, db_.t[:].rearrange("p a b c d -> p (a b c d)"), dlb_d.ap()[hg_ - 4], db_,
                                       writes=[db_])
                        if ui == 0:
                            table_load(s, hg, ui)
                        nxt_unit = (s, hg + 1) if hg < 7 else (("P", 0) if s == "S" else None)
                        if nxt_unit is not None:
                            table_load(nxt_unit[0], nxt_unit[1], ui + 1)
                        if hg < 4:
                            nb = nab[ui % 2]
                        else:
                            db = dlb[ui % 2]
                        for q0 in range(lo, hi, 2048):
                            qt = qts[cnt["st"] % 2]
                            kt = kts[cnt["st"] % 2]
                            cnt["st"] += 1
                            kb.dma(sp, qt.t[:], sg["QT"].ap()[h0:h0 + 2, :, PAD + q0:PAD + q0 + 2048].rearrange("h d t -> d h t"),
                                   qt, writes=[qt])
                            kb.dma(sp, kt.t[:], sg["KT"].ap()[h0:h0 + 2, :, q0:q0 + 4096].rearrange("h d t -> d h t"),
                                   kt, writes=[kt])
                            if hg < 4:
                                vb = vbuf[cnt["v"] % 2]
                                cnt["v"] += 1
                                r0 = PAD + q0 - 512
                                kb.dma(sp, vb.t[:, 0:24, :],
                                       sg["V"].ap()[r0:r0 + 24 * 128, h0 * VW:(h0 + 2) * VW].rearrange("(n p) c -> p n c", p=128),
                                       vb, writes=[vb])
                                for b in range(16):
                                    lq0 = q0 // GRID_W + 2 * b
                                    pairs = plan[lq0]
                                    ost = osts[cnt["job"] % 3]
                                    cnt["job"] += 1
                                    ubs = [u_banks[(cnt["ub"] + hh) % 4] for hh in range(2)]
                                    cnt["ub"] += 2
                                    fin = na_fin(ubs, ost, sg, q0 + 128 * b, h0)
                                    for pi, (ktr, slot) in enumerate(pairs):
                                        ktok = ktr * GRID_W
                                        koff = ktok - (q0 - 1024)
                                        vn = (ktok - (q0 - 512)) // 128
                                        assert 0 <= vn < 24 and 0 <= koff <= 4096 - 128
                                        tile_pair2([kt.t[:, hh, koff:koff + 128] for hh in range(2)],
                                                   [qt.t[:, hh, 128 * b:128 * b + 128] for hh in range(2)],
                                                   nb.t[:, slot, :, :].rearrange("p h q -> p (h q)"), None, None,
                                                   [vb.t[:, vn, hh * VW:hh * VW + HD + 1] for hh in range(2)],
                                                   ubs, pi == 0, pi == len(pairs) - 1, [kt, qt, nb], vb, fin)
                            else:
                                hd = hg - 4
                                for bi, dil in enumerate(DILS):
                                    nblk = 2048 // (128 * dil)
                                    ntile = nblk + 1
                                    vb = vbuf[cnt["v"] % 2]
                                    cnt["v"] += 1
                                    mA = q0 // dil - 64
                                    for r in range(dil):
                                        base = PAD + mA * dil + r
                                        src = sg["V"].ap()[base:base + dil * (128 * ntile - 1) + 1:dil, h0 * VW:(h0 + 2) * VW]
                                        kb.dma(sp, vb.t[:, r * ntile:(r + 1) * ntile, :],
                                               src.rearrange("(n p) c -> p n c", p=128), vb, writes=[vb])
                                    kb0, kcnt = kbase[bi]
                                    for r in range(dil):
                                        for blk in range(nblk):
                                            ust = usts[cnt["job"] % 3]
                                            cnt["job"] += 1
                                            qc0 = blk * 128 * dil + r
                                            ubs = [u_banks[(cnt["ub"] + hh) % 4] for hh in range(2)]
                                            cnt["ub"] += 2
                                            fin = dil_fin(ubs, ust, sg, bi, q0 + blk * 128 * dil + r, dil, hd)
                                            for ktile in range(2):
                                                kc0 = (-64 + 128 * (blk + ktile)) * dil + r + 1024
                                                m_t = mA + 128 * (blk + ktile)
                                                assert (m_t + 64) % 64 == 0 and 0 <= (m_t + 64) // 64 < kcnt
                                                kmcol = kb0 + r * kcnt + (m_t + 64) // 64
                                                assert 0 <= kc0 and kc0 + 127 * dil < 4096
                                                tile_pair2([kt.t[:, hh, kc0:kc0 + 127 * dil + 1:dil] for hh in range(2)],
                                                           [qt.t[:, hh, qc0:qc0 + 127 * dil + 1:dil] for hh in range(2)],
                                                           db.t[:, bi, ktile, :, :].rearrange("p h q -> p (h q)"),
                                                           kmt[s].t[:, kmcol:kmcol + 1], kmt[s],
                                                           [vb.t[:, r * ntile + blk + ktile, hh * VW:hh * VW + HD + 1] for hh in range(2)],
                                                           ubs, ktile == 0, ktile == 1, [kt, qt, db], vb, fin)
                    flush(0)
                flush(0)
                kb.barrier()
            kb.retire(mark)

        dense_phase(None, False, True, 0, False)
        emit_casts(None, layer=0)
        if DEBUG_STOP != "A0":
            attn_phase(0)
            if DEBUG_STOP != "B0":
                dense_phase(0, True, True, 1, False)
                if DEBUG_STOP != "C0":
                    attn_phase(1)
                    dense_phase(1, True, False, None, True)
        kb.barrier()
    return nc


_NC_CACHE = {}


def kernel(x_prompt, x_sample, w_in, w_out, g_attn, g_na, g_dil, rpb_na, t5_table, g_ffn, w_gate, w_up, w_down,
           g_final):
    f = lambda a: np.ascontiguousarray(np.asarray(a, dtype=np.float32))
    x_prompt, x_sample = f(x_prompt), f(x_sample)
    w_in, w_out, w_gate, w_up, w_down = f(w_in), f(w_out), f(w_gate), f(w_up), f(w_down)
    g_attn, g_ffn, g_final = f(g_attn), f(g_ffn), f(g_final)
    rpb_na, t5_table = f(rpb_na), f(t5_table)
    g_mix = np.ascontiguousarray(np.concatenate([f(g_na), f(g_dil)], axis=1))
    if "nc" not in _NC_CACHE:
        _NC_CACHE["nc"] = build_program()
    nc = _NC_CACHE["nc"]
    ident = np.eye(128, dtype=np.float32)
    dlb = np.stack([_dil_table(t5_table, hd).reshape(128, -1) for hd in range(4)])
    xpad = np.zeros((2048 + 16384 + 2048, D), np.float32)
    xpad[2048:2048 + 16384] = x_prompt[0]
    nabS = np.stack([np.stack([_na_table("S", 0, rpb_na[l], hg).reshape(128, -1) for hg in range(4)])
                     for l in range(DEPTH)])
    kmS = _km_table("S", 0)
    in_maps = []
    for c in range(NCORES):
        nabP = np.stack([np.stack([_na_table("P", c, rpb_na[l], hg).reshape(128, -1) for hg in range(4)])
                         for l in range(DEPTH)])
        in_maps.append({
            "xs": x_sample[c], "xp": np.ascontiguousarray(xpad[2048 * c:2048 * c + LP]),
            "w_in": w_in, "w_out": w_out, "w_gate": w_gate, "w_up": w_up, "w_down": w_down,
            "g_attn": g_attn, "g_ffn": g_ffn, "g_mix": g_mix, "g_final": g_final.reshape(1, D),
            "identd": ident, "zerosd": np.zeros((128, NH * VW), np.float32), "nabS": nabS, "nabP": nabP, "dlb": dlb, "kmS": kmS, "kmP": _km_table("P", c),
        })
    res = run_bass_kernel_spmd(nc, in_maps, core_ids=list(range(NCORES)))
    y_sample = np.stack([np.asarray(res.results[c]["ys"], dtype=np.float32) for c in range(NCORES)])
    y_prompt = np.concatenate([np.asarray(res.results[c]["yp"], dtype=np.float32) for c in range(NCORES)])[None]
    return (y_prompt, y_sample)
```

```python
import contextlib
import math
import numpy as np
import concourse.bass as bass
import concourse.mybir as mybir
from concourse.bass_utils import run_bass_kernel_spmd

F32 = mybir.dt.float32
BF16 = mybir.dt.bfloat16
AF = mybir.ActivationFunctionType
ALU = mybir.AluOpType

NCORES = 8
D = 2048
DC = 16
NH = 16
HD = 128
DFF = 5632
FC = 44
DEPTH = 2
T = 512
LS = 4096
LP = 6144
PAD = 1024
VW = 132
GRID_W = 64
SCALE = 1.0 / math.sqrt(HD)
EPS = 1e-6
NEG = -1e30
DILS = (1, 4, 16)
FF_PARTS = [(0, 8), (8, 8), (16, 8), (24, 8), (32, 8), (40, 4)]

DEBUG_STOP = None


def _t5_bucket(rel):
    nb, md = 32, 2048
    half = nb // 2
    max_exact = half // 2
    n = np.abs(rel)
    large = max_exact + (np.log(np.maximum(n, max_exact) / max_exact)
                         / np.log(md / max_exact) * (half - max_exact)).astype(np.int32)
    large = np.minimum(large, half - 1)
    return (rel > 0).astype(np.int32) * half + np.where(n < max_exact, n, large).astype(np.int32)


def _seg_rows(seg, core):
    if seg == "S":
        return 0, LS // GRID_W, LS // GRID_W
    return 32 * core - 32, 256, LP // GRID_W


def _na_included(seg, core, lq, lk):
    roff, rows, _ = _seg_rows(seg, core)
    gq, gk = lq + roff, lk + roff
    if not (0 <= gk < rows):
        return False
    if 0 <= gq < rows:
        rs = min(max(gq - 4, 0), rows - 8)
    else:
        rs = gq - 4
    return rs <= gk < rs + 8


def _na_plan(seg):
    _, _, lrows = _seg_rows(seg, 0)
    slots = {}
    plan = {}
    qrows = range(0, lrows, 2) if seg == "S" else range(16, 80, 2)
    for lq0 in qrows:
        lst = []
        for kt in range(lq0 - 8, lq0 + 10, 2):
            if kt < -8 or kt + 1 >= lrows + 8:
                continue
            pats = []
            anyinc = False
            for c in range(NCORES):
                pat = tuple(_na_included(seg, c, lq0 + b, kt + a) for a in range(2) for b in range(2))
                anyinc = anyinc or any(pat)
                pats.append(pat)
            if not anyinc:
                continue
            key = (kt - lq0, tuple(pats))
            if key not in slots:
                slots[key] = len(slots)
            lst.append((kt, slots[key]))
        plan[lq0] = lst
    return plan, slots


_NA_PLANS = {s: _na_plan(s) for s in ("S", "P")}


def _na_table(seg, core, rpb_l, hg):
    _, slots = _NA_PLANS[seg]
    ns = len(slots)
    c = np.arange(GRID_W)
    cs = np.clip(c - 8, 0, GRID_W - 16)
    col_ok = (c[None, :] >= cs[:, None]) & (c[None, :] < cs[:, None] + 16)
    dc_idx = np.clip(c[None, :] - c[:, None] + 15, 0, 30)
    out = np.full((128, ns, 2, 128), NEG, np.float32)
    for (dr0, pats), s in slots.items():
        pat = pats[core]
        for a in range(2):
            for b in range(2):
                if not pat[a * 2 + b]:
                    continue
                dr = dr0 + a - b
                if abs(dr) > 7:
                    continue
                for hh in range(2):
                    vals = rpb_l[dr + 7][:, 2 * hg + hh][dc_idx]
                    blk = np.where(col_ok, vals, np.float32(NEG))
                    out[a * 64:(a + 1) * 64, s, hh, b * 64:(b + 1) * 64] = blk.T
    return out


def _dil_table(t5, hd):
    out = np.full((128, 3, 2, 2, 128), NEG, np.float32)
    i = np.arange(128)[:, None]
    j = np.arange(128)[None, :]
    for bi, dil in enumerate(DILS):
        for kt in range(2):
            off = (i - 64 + 128 * kt) - j
            ok = np.abs(off) <= 64
            bidx = _t5_bucket(dil * np.clip(off, -64, 64))
            for hh in range(2):
                vals = t5[:, 2 * hd + hh][bidx]
                out[:, bi, kt, hh, :] = np.where(ok, vals, np.float32(NEG))
    return out


def _km_layout(L):
    base = []
    n = 0
    for dil in DILS:
        cnt = L // (64 * dil) + 1
        base.append((n, cnt))
        n += dil * cnt
    return base, n


def _km_table(seg, core):
    L = LS if seg == "S" else LP
    base, ncol = _km_layout(L)
    out = np.zeros((128, ncol), np.float32)
    goff, gl = (0, LS) if seg == "S" else (2048 * core - 2048, 16384)
    for bi, dil in enumerate(DILS):
        b0, cnt = base[bi]
        for r in range(dil):
            for ti in range(cnt):
                tok = (64 * ti - 64 + np.arange(128)) * dil + r
                ok = (tok >= 0) & (tok < L) & (tok + goff >= 0) & (tok + goff < gl)
                out[:, b0 + r * cnt + ti] = np.where(ok, np.float32(0.0), np.float32(NEG))
    return out


class Tok:
    __slots__ = ("sem", "val")

    def __init__(self, sem, val):
        self.sem = sem
        self.val = val


class Eng:
    def __init__(self, kb, name, handle, selfwait):
        self.name = name
        self.h = handle
        self.sem = kb.new_sem("e_" + name)
        self.cnt = 0
        self.waited = {}
        self.selfwait = selfwait

    def wait(self, tok):
        if tok is None:
            return
        if tok.sem is self.sem and not self.selfwait:
            return
        k = tok.sem.num
        if self.waited.get(k, 0) >= tok.val:
            return
        self.h.wait_ge(tok.sem, tok.val)
        self.waited[k] = tok.val

    def finish(self, ins):
        ins.then_inc(self.sem, 1)
        self.cnt += 1
        return Tok(self.sem, self.cnt)


class Buf:
    def __init__(self, kb, name, t=None):
        self.kb = kb
        self.name = name
        self.t = t
        self.w = None
        self.r = {}
        self.dsem = None
        self.dcnt = 0
        kb.bufs.append(self)

    def dma_sem(self):
        if self.dsem is None:
            if self.kb.free_dsems:
                self.dsem, self.dcnt = self.kb.free_dsems.pop()
            else:
                self.dsem = self.kb.new_sem("d%d" % self.kb.nsem)
        return self.dsem


class KB:
    def __init__(self, nc, es):
        self.nc = nc
        self.es = es
        self.bufs = []
        self.nsem = 0
        self.free_dsems = []
        self.pe = Eng(self, "pe", nc.tensor, False)
        self.act = Eng(self, "act", nc.scalar, True)
        self.dve = Eng(self, "dve", nc.vector, True)
        self.pool = Eng(self, "pool", nc.gpsimd, True)
        self.sp = Eng(self, "sp", nc.sync, False)
        self.engs = [self.pe, self.act, self.dve, self.pool, self.sp]

    def new_sem(self, name):
        self.nsem += 1
        return self.es.enter_context(self.nc.semaphore(name))

    def sb(self, name, shape, dt, es=None):
        self.nt = getattr(self, "nt", 0) + 1
        name = "%s_%d" % (name, self.nt)
        t = (es or self.es).enter_context(self.nc.sbuf_tensor(name, shape, dt))
        return Buf(self, name, t)

    def ps(self, name, shape, dt, es=None):
        self.nt = getattr(self, "nt", 0) + 1
        name = "%s_%d" % (name, self.nt)
        t = (es or self.es).enter_context(self.nc.psum_tensor(name, shape, dt))
        return Buf(self, name, t)

    def op(self, eng, fn, reads=(), writes=()):
        for b in reads:
            eng.wait(b.w)
        for b in writes:
            eng.wait(b.w)
            for t in b.r.values():
                eng.wait(t)
        ins = fn()
        tok = eng.finish(ins)
        for b in reads:
            b.r[eng.name] = tok
        for b in writes:
            b.w = tok
            b.r = {}
        return tok

    def dma(self, q, out_ap, in_ap, owner, reads=(), writes=()):
        for b in reads:
            q.wait(b.w)
        sem = owner.dma_sem()
        for b in writes:
            if not (b.w is not None and b.w.sem is sem):
                q.wait(b.w)
            for t in b.r.values():
                q.wait(t)
        q.h.dma_start(out=out_ap, in_=in_ap).then_inc(sem, 16)
        owner.dcnt += 16
        tok = Tok(sem, owner.dcnt)
        for b in reads:
            b.r["dma_" + owner.name] = tok
        for b in writes:
            b.w = tok
            b.r = {}
        return tok

    def barrier(self, only=None):
        if only is None:
            toks = [Tok(e.sem, e.cnt) for e in self.engs if e.cnt > 0]
            toks += [Tok(b.dsem, b.dcnt) for b in self.bufs if b.dsem is not None and b.dcnt > 0]
        else:
            toks = [Tok(b.dsem, b.dcnt) for b in only if b.dsem is not None and b.dcnt > 0]
        for e in self.engs:
            for t in toks:
                if t.sem is e.sem:
                    continue
                e.wait(t)

    def retire(self, mark):
        for b in self.bufs[mark:]:
            if b.dsem is not None:
                self.free_dsems.append((b.dsem, b.dcnt))
        del self.bufs[mark:]


def bcast_last(ap, n):
    return bass.AP(ap.tensor, ap.offset, [list(x) for x in ap.ap] + [[0, n]])


def build_program(debug_stop=None, debug_out=()):
    nc = bass.Bass("TRN2", target_bir_lowering=False)
    es = contextlib.ExitStack()
    DEBUG_STOP = debug_stop
    dram = lambda n, s, dt, kind: nc.dram_tensor(n, list(s), dt, kind=("ExternalOutput" if n in debug_out else kind))
    xs = dram("xs", [LS, D], F32, "ExternalInput")
    xp = dram("xp", [LP, D], F32, "ExternalInput")
    w_in = dram("w_in", [DEPTH, D, 3 * D], F32, "ExternalInput")
    w_out = dram("w_out", [DEPTH, D, D], F32, "ExternalInput")
    w_gate = dram("w_gate", [DEPTH, D, DFF], F32, "ExternalInput")
    w_up = dram("w_up", [DEPTH, D, DFF], F32, "ExternalInput")
    w_down = dram("w_down", [DEPTH, DFF, D], F32, "ExternalInput")
    g_attn = dram("g_attn", [DEPTH, D], F32, "ExternalInput")
    g_ffn = dram("g_ffn", [DEPTH, D], F32, "ExternalInput")
    g_mix = dram("g_mix", [DEPTH, D], F32, "ExternalInput")
    g_final = dram("g_final", [1, D], F32, "ExternalInput")
    identd = dram("identd", [128, 128], F32, "ExternalInput")
    zerosd = dram("zerosd", [128, NH * VW], F32, "ExternalInput")
    nsl = {s: len(_NA_PLANS[s][1]) for s in ("S", "P")}
    nab_d = {s: dram("nab" + s, [DEPTH, 4, 128, nsl[s] * 256], F32, "ExternalInput") for s in ("S", "P")}
    dlb_d = dram("dlb", [4, 128, 3 * 2 * 2 * 128], F32, "ExternalInput")
    kml = {"S": _km_layout(LS), "P": _km_layout(LP)}
    km_d = {s: dram("km" + s, [128, kml[s][1]], F32, "ExternalInput") for s in ("S", "P")}
    ys = dram("ys", [LS, D], F32, "ExternalOutput")
    yp = dram("yp", [2048, D], F32, "ExternalOutput")
    wb = {
        "in": dram("wb_in", [DEPTH, D, 3 * D], BF16, "Internal"),
        "out": dram("wb_out", [DEPTH, D, D], BF16, "Internal"),
        "gate": dram("wb_gate", [DEPTH, D, DFF], BF16, "Internal"),
        "up": dram("wb_up", [DEPTH, D, DFF], BF16, "Internal"),
        "down": dram("wb_down", [DEPTH, DFF, D], BF16, "Internal"),
    }
    wsrc = {"in": w_in, "out": w_out, "gate": w_gate, "up": w_up, "down": w_down}
    SEG = {}
    for s, L, xin in (("S", LS, xs), ("P", LP, xp)):
        SEG[s] = dict(
            name=s, L=L, xin=xin,
            QT=dram("QT" + s, [NH, HD, L + 2 * PAD], BF16, "Internal"),
            KT=dram("KT" + s, [NH, HD, L + 2 * PAD], BF16, "Internal"),
            V=dram("V" + s, [L + 2 * PAD, NH * VW], BF16, "Internal"),
            ONA=dram("ONA" + s, [L, 8 * HD], F32, "Internal"),
            UD=[dram("UD%d%s" % (b, s), [L, 8 * VW], F32, "Internal") for b in range(3)],
            X1=dram("X1" + s, [L, D], F32, "Internal"),
        )
    RANGE_A0 = {"S": (0, LS), "P": (0, LP)}
    RANGE_L = [{"S": (0, LS), "P": (1024, 5120)}, {"S": (0, LS), "P": (2048, 4096)}]

    with es:
        kb = KB(nc, es)
        pe, act, dve, pool, sp = kb.pe, kb.act, kb.dve, kb.pool, kb.sp
        ident = kb.sb("ident", [128, 128], BF16)
        epsb = kb.sb("epsb", [128, 1], F32)
        stats = [[kb.sb("stats%d_%d" % (i, j), [128, 16], F32) for j in range(4)] for i in range(4)]
        kmt = {s: kb.sb("kmt" + s, [128, kml[s][1]], F32) for s in ("S", "P")}
        tp_banks = []
        ps_banks = []
        es.enter_context(nc.Block())

        kb.dma(pool, ident.t[:], identd.ap(), ident, writes=[ident])
        kb.op(dve, lambda: nc.vector.memset(epsb.t[:], EPS), writes=[epsb])
        for s in ("S", "P"):
            kb.dma(sp, kmt[s].t[:], km_d[s].ap(), kmt[s], writes=[kmt[s]])
        wready = {}
        order = [("in", 0), ("out", 0), ("gate", 0), ("up", 0), ("down", 0),
                 ("in", 1), ("out", 1), ("gate", 1), ("up", 1), ("down", 1)]
        n_early = 8 + 8 + 8 + 22 + 8
        cast_q = []
        for nm, l in order:
            b = Buf(kb, "wc_%s%d" % (nm, l))
            wready[(nm, l)] = b
            rows = wsrc[nm].ap().shape[1]
            step = 256
            for r0 in range(0, rows, step):
                cast_q.append((nm, l, r0, step, b))

        cast_q = [c for c in cast_q if not (c[0] == "in" and c[1] == 0)]
        win0 = []
        for j in range(12):
            b = Buf(kb, "wc_in0_%d" % j)
            win0.append(b)
            kb.dma(pool, wb["in"].ap()[0, :, j * 512:(j + 1) * 512], wsrc["in"].ap()[0, :, j * 512:(j + 1) * 512], b,
                   writes=[b])
        n_total = len(cast_q)

        def emit_casts(n=None, layer=None):
            k = 0
            while cast_q and (n is None or k < n) and (layer is None or len(cast_q) > n_total - n_early):
                nm, l, r0, step, b = cast_q.pop(0)
                kb.dma(pool, wb[nm].ap()[l, r0:r0 + step, :], wsrc[nm].ap()[l, r0:r0 + step, :], b, writes=[b])
                k += 1

        def emit_zero_fill():
            zb = Buf(kb, "zfill")
            sgS = SEG["S"]
            for side in (0, PAD + LS):
                src = bass.AP(zerosd, 0, [[0, NH], [NH * VW, 128], [1, PAD]])
                kb.dma(pool, sgS["KT"].ap()[:, :, side:side + PAD], src, zb, writes=[zb])
                for n in range(PAD // 128):
                    kb.dma(pool, sgS["V"].ap()[side + n * 128: side + (n + 1) * 128, :], zerosd.ap(), zb, writes=[zb])

        evac_rr = [0]

        def evac_engine():
            evac_rr[0] += 1
            return act if (evac_rr[0] & 1) else dve

        def copy_on(eng, out_ap, in_ap, reads, writes, scale=None):
            if eng is act:
                if scale is None:
                    return kb.op(act, lambda: nc.scalar.copy(out=out_ap, in_=in_ap), reads, writes)
                return kb.op(act, lambda: nc.scalar.mul(out=out_ap, in_=in_ap, mul=scale), reads, writes)
            if scale is None:
                return kb.op(dve, lambda: nc.vector.tensor_copy(out=out_ap, in_=in_ap), reads, writes)
            return kb.op(dve, lambda: nc.vector.tensor_scalar_mul(out=out_ap, in0=in_ap, scalar1=scale), reads, writes)

        bank_rr = [0]

        def next_bank():
            b = ps_banks[bank_rr[0] % 6]
            bank_rr[0] += 1
            return b

        def rms_stats(src_ap, junk_buf, junk_ap, src_bufs, st, col, nfeat):
            kb.op(act, lambda: nc.scalar.activation(out=junk_ap, in_=src_ap, func=AF.Square,
                                                    accum_out=st.t[:, col:col + 1]),
                  reads=src_bufs, writes=[junk_buf, st])
            kb.op(act, lambda: nc.scalar.activation(out=st.t[:, col:col + 1], in_=st.t[:, col:col + 1], func=AF.Sqrt,
                                                    scale=1.0 / nfeat, bias=epsb.t[:, 0:1]),
                  reads=[epsb], writes=[st])
            kb.op(dve, lambda: nc.vector.reciprocal(out=st.t[:, col + 1:col + 2], in_=st.t[:, col:col + 1]),
                  writes=[st])

        def transposes(src, s, dstT):
            for half in range(2):
                tpb = tp_banks[half]
                tpv = tpb.t[:]

                def mk(half=half, tpv=tpv):
                    ins = None
                    for k in range(8):
                        kc = half * 8 + k
                        ins = nc.tensor.transpose(out=tpv[:, k * 128:(k + 1) * 128],
                                                  in_=src.t[:, kc * 128:(kc + 1) * 128], identity=ident.t[:])
                    return ins
                kb.op(pe, mk, reads=[src, ident], writes=[tpb])
                copy_on(evac_engine(), dstT.t[:, half * 8:(half + 1) * 8, s * 128:(s + 1) * 128],
                        tpv.rearrange("p (a b) -> p a b", a=8), [tpb], [dstT])

        def dense_phase(layer_prev, do_c, do_a, a_layer, last):
            mark = len(kb.bufs)
            with contextlib.ExitStack() as esd:
                tp_banks[:] = [kb.ps("tpb%d" % i, [128, 1024], BF16, esd) for i in range(2)]
                ps_banks[:] = [kb.ps("psb%d" % i, [128, 512], F32, esd) for i in range(6)]
                xt = kb.sb("xt", [128, 4, D], F32, esd)
                xts = [Buf(kb, "xts%d" % i, xt.t) for i in range(4)]

                def x_load(src, t0):
                    for s_ in range(4):
                        kb.dma(sp, xt.t[:, s_, :], src.ap()[t0 + s_ * 128:t0 + (s_ + 1) * 128, :], xts[s_], writes=[xts[s_]])

                def x_store(dst, t0):
                    for s_ in range(4):
                        kb.dma(pool, dst.ap()[t0 + s_ * 128:t0 + (s_ + 1) * 128, :], xt.t[:, s_, :], xts[s_], reads=[xts[s_]])
                hTs = [kb.sb("hT%d" % i, [128, DC, T], BF16, esd) for i in range(2)]
                hpre = [kb.sb("hpre%d" % i, [128, D], BF16, esd) for i in range(2)]
                wsl = [kb.sb("wsl%d" % i, [128, DC, 512], BF16, esd) for i in range(3)]
                qkst = [kb.sb("qkst%d" % i, [128, 4, T], BF16, esd) for i in range(2)]
                vst = [kb.sb("vst%d" % i, [128, 4, VW], BF16, esd) for i in range(4)]
                gA = kb.sb("gA", [128, D], F32, esd)
                if do_c:
                    actT = [kb.sb("actT%d" % i, [128, 8, T], BF16, esd) for i in range(2)]
                    sgt = [kb.sb("sgt%d" % i, [128, T], F32, esd) for i in range(2)]
                    onab = [kb.sb("onab%d" % i, [128, 8 * HD], F32, esd) for i in range(2)]
                    udb = [kb.sb("udb%d" % i, [128, 8 * VW], F32, esd) for i in range(3)]
                    gF = kb.sb("gF", [128, D], F32, esd)
                    gM = kb.sb("gM", [128, D], F32, esd)
                    ost = [kb.sb("ost%d" % i, [128, 1024], F32, esd) for i in range(2)]
                wrr = [0]

                def wslot():
                    b = wsl[wrr[0] % 3]
                    wrr[0] += 1
                    return b

                for v in vst:
                    kb.op(dve, lambda v=v: nc.vector.memset(v.t[:], 0.0), writes=[v])
                    kb.op(dve, lambda v=v: nc.vector.memset(v.t[:, :, 128:129], 1.0), writes=[v])
                if do_a:
                    kb.dma(sp, gA.t[:], g_attn.ap()[a_layer:a_layer + 1, :].partition_broadcast(128), gA, writes=[gA])
                elif last:
                    kb.dma(sp, gA.t[:], g_final.ap().partition_broadcast(128), gA, writes=[gA])
                if do_c:
                    kb.dma(sp, gF.t[:], g_ffn.ap()[layer_prev:layer_prev + 1, :].partition_broadcast(128), gF, writes=[gF])
                    kb.dma(sp, gM.t[:], g_mix.ap()[layer_prev:layer_prev + 1, :].partition_broadcast(128), gM, writes=[gM])

                def wload(slot, nm, l, rows, r0, c0, ncols, dst_c0=0, nk=DC):
                    src = wb[nm].ap()[l, r0:r0 + nk * 128, c0:c0 + ncols].rearrange("(kc p) n -> p kc n", p=128)
                    dep = win0[c0 // 512] if (nm == "in" and l == 0) else wready[(nm, l)]
                    kb.dma(sp, slot.t[:, 0:nk, dst_c0:dst_c0 + ncols], src, slot, reads=[dep], writes=[slot])

                def norm_events(gt, dstT, sts):
                    def prep(s):
                        hp = hpre[s % 2]
                        st = sts[s]
                        rms_stats(xt.t[:, s, :], hp, hp.t[:], [xts[s]], st, 0, D)
                        kb.op(dve, lambda: nc.vector.scalar_tensor_tensor(
                            out=hp.t[:], in0=xt.t[:, s, :], scalar=st.t[:, 1:2], in1=gt.t[:],
                            op0=ALU.mult, op1=ALU.mult), reads=[xts[s], st, gt], writes=[hp])

                    def tr(s):
                        transposes(hpre[s % 2], s, dstT)
                    return [lambda: (prep(0), prep(1)),
                            lambda: (tr(0), prep(2)),
                            lambda: (tr(1), prep(3)),
                            lambda: tr(2),
                            lambda: tr(3)]

                def norm_T(gt, dstT, sts):
                    for ev in norm_events(gt, dstT, sts):
                        ev()

                def qkv(l, sg, t0, hT, kv_only=False, events=(), cast_spread=False):
                    jl = [j for j in range(12) if not (kv_only and (j // 2) % 3 == 0)]
                    events = list(events)
                    ne = len(events)
                    at = {int(round((e + 1) * len(jl) / (ne + 1))): e for e in range(ne)}
                    assert len(at) == ne
                    for ji, j in enumerate(jl):
                        if ji in at:
                            events[at[ji]]()
                        if cast_spread and ji % 4 == 1:
                            emit_casts(1, layer=0)
                        slot = wslot()
                        wload(slot, "in", l, D, 0, j * 512, 512)
                        kind = ("q", "k", "v")[(j // 2) % 3]
                        h0 = (j % 2) * 4 + (8 if j >= 6 else 0)
                        if kind in ("q", "k"):
                            st_ = qkst[j % 2]
                            for hh in range(4):
                                bank = next_bank()

                                def mk(hh=hh, bank=bank, slot=slot):
                                    ins = None
                                    for kc in range(DC):
                                        ins = nc.tensor.matmul(bank.t[:], slot.t[:, kc, hh * 128:(hh + 1) * 128],
                                                               hT.t[:, kc, :], start=(kc == 0), stop=(kc == DC - 1))
                                    return ins
                                kb.op(pe, mk, reads=[slot, hT], writes=[bank])
                                copy_on(evac_engine(), st_.t[:, hh, :], bank.t[:], [bank], [st_],
                                        scale=(SCALE if kind == "q" else None))
                            dst = sg["QT" if kind == "q" else "KT"].ap()[h0:h0 + 4, :, PAD + t0:PAD + t0 + T]
                            kb.dma(pool, dst.rearrange("h d t -> d h t"), st_.t[:], st_, reads=[st_])
                        else:
                            for s in range(4):
                                bank = next_bank()

                                def mk(s=s, bank=bank, slot=slot):
                                    ins = None
                                    for kc in range(DC):
                                        ins = nc.tensor.matmul(bank.t[:], hT.t[:, kc, s * 128:(s + 1) * 128],
                                                               slot.t[:, kc, :], start=(kc == 0), stop=(kc == DC - 1))
                                    return ins
                                kb.op(pe, mk, reads=[slot, hT], writes=[bank])
                                v = vst[s]
                                copy_on(evac_engine(), v.t[:, :, 0:128], bank.t[:].rearrange("p (h d) -> p h d", h=4),
                                        [bank], [v])
                                r0 = PAD + t0 + s * 128
                                kb.dma(pool, sg["V"].ap()[r0:r0 + 128, h0 * VW:(h0 + 4) * VW],
                                       v.t[:].rearrange("p h c -> p (h c)"), v, reads=[v])

                def xadd(bank, s, q):
                    kb.op(dve, lambda: nc.vector.tensor_tensor(out=xt.t[:, s, q * 512:(q + 1) * 512], in0=bank.t[:],
                                                               in1=xt.t[:, s, q * 512:(q + 1) * 512], op=ALU.add),
                          reads=[bank], writes=[xts[s]])

                def mix_loads(sg, tok0, i):
                    kb.dma(sp, onab[i % 2].t[:], sg["ONA"].ap()[tok0:tok0 + 128, :], onab[i % 2], writes=[onab[i % 2]])
                    for b in range(3):
                        kb.dma(sp, udb[b].t[:], sg["UD"][b].ap()[tok0:tok0 + 128, :], udb[b], writes=[udb[b]])

                def mix_stage(sg, t0, s, mixT):
                    st = stats[1][s]
                    ona = onab[s % 2]
                    mix_loads(sg, t0 + s * 128, s)
                    kb.op(pool, lambda: nc.gpsimd.tensor_tensor(out=udb[0].t[:], in0=udb[0].t[:], in1=udb[1].t[:],
                                                                op=ALU.add), reads=[udb[1]], writes=[udb[0]])
                    kb.op(pool, lambda: nc.gpsimd.tensor_tensor(out=udb[0].t[:], in0=udb[0].t[:], in1=udb[2].t[:],
                                                                op=ALU.add), reads=[udb[2]], writes=[udb[0]])
                    u0v = udb[0].t[:].rearrange("p (h c) -> p h c", h=8)
                    rl = st.t[:, 8:16]
                    kb.op(dve, lambda: nc.vector.tensor_scalar_add(out=rl, in0=u0v[:, :, 128], scalar1=1e-30),
                          reads=[udb[0]], writes=[st])
                    kb.op(dve, lambda: nc.vector.reciprocal(out=rl, in_=rl), writes=[st])
                    odil = udb[1].t[:, 0:1024]
                    kb.op(dve, lambda: nc.vector.tensor_tensor(out=odil.rearrange("p (h d) -> p h d", h=8),
                                                               in0=u0v[:, :, 0:128], in1=bcast_last(rl, 128),
                                                               op=ALU.mult), reads=[udb[0], st], writes=[udb[1]])
                    hp = hpre[s % 2]
                    rms_stats(ona.t[:], hp, hp.t[:, 0:1024], [ona], st, 0, 1024)
                    rms_stats(odil, hp, hp.t[:, 1024:2048], [udb[1]], st, 2, 1024)
                    kb.op(dve, lambda: nc.vector.scalar_tensor_tensor(
                        out=hp.t[:, 0:1024], in0=ona.t[:], scalar=st.t[:, 1:2], in1=gM.t[:, 0:1024],
                        op0=ALU.mult, op1=ALU.mult), reads=[ona, st, gM], writes=[hp])
                    kb.op(dve, lambda: nc.vector.scalar_tensor_tensor(
                        out=hp.t[:, 1024:2048], in0=odil, scalar=st.t[:, 3:4], in1=gM.t[:, 1024:2048],
                        op0=ALU.mult, op1=ALU.mult), reads=[udb[1], st, gM], writes=[hp])


                def mix_T(s, mixT):
                    transposes(hpre[s % 2], s, mixT)

                def mix_events(nxt):
                    mt = hTs[0]
                    return [lambda: mix_stage(nxt[0], nxt[1], 0, mt),
                            lambda: (mix_T(0, mt), mix_stage(nxt[0], nxt[1], 1, mt)),
                            lambda: (mix_T(1, mt), mix_stage(nxt[0], nxt[1], 2, mt)),
                            lambda: (mix_T(2, mt), mix_stage(nxt[0], nxt[1], 3, mt)),
                            lambda: mix_T(3, mt)]

                def c_tile(l, sg, t0, ffn_events=()):
                    xsrc = sg["xin"] if l == 0 else sg["X1"]
                    mixT, hTf = hTs[0], hTs[1]
                    for q in range(4):
                        slot = wslot()
                        wload(slot, "out", l, D, 0, q * 512, 512)
                        if q == 0 and not xpref[0]:
                            x_load(xsrc, t0)
                        xpref[0] = False if q == 0 else xpref[0]
                        for s in range(4):
                            bank = next_bank()

                            def mk(s=s, bank=bank, slot=slot):
                                ins = None
                                for kc in range(DC):
                                    ins = nc.tensor.matmul(bank.t[:], mixT.t[:, kc, s * 128:(s + 1) * 128],
                                                           slot.t[:, kc, :], start=(kc == 0), stop=(kc == DC - 1))
                                return ins
                            kb.op(pe, mk, reads=[slot, mixT], writes=[bank])
                            xadd(bank, s, q)
                    norm_T(gF, hTf, stats[2])
                    ffn_events = list(ffn_events)

                    def gu_pair(pi, jp):
                        j0, nj = FF_PARTS[pi]
                        at = actT[pi % 2]
                        slot = wslot()
                        wload(slot, "gate", l, D, 0, (j0 + jp) * 128, 256, 0)
                        wload(slot, "up", l, D, 0, (j0 + jp) * 128, 256, 256)
                        for c in range(2):
                            bg, bu = next_bank(), next_bank()

                            def mk(c=c, bg=bg, bu=bu, slot=slot):
                                ins = None
                                for kc in range(DC):
                                    ins = nc.tensor.matmul(bg.t[:], slot.t[:, kc, c * 128:(c + 1) * 128],
                                                           hTf.t[:, kc, :], start=(kc == 0), stop=(kc == DC - 1))
                                for kc in range(DC):
                                    ins = nc.tensor.matmul(bu.t[:], slot.t[:, kc, 256 + c * 128:256 + (c + 1) * 128],
                                                           hTf.t[:, kc, :], start=(kc == 0), stop=(kc == DC - 1))
                                return ins
                            kb.op(pe, mk, reads=[slot, hTf], writes=[bg, bu])
                            sg_ = sgt[(jp + c) % 2]
                            kb.op(act, lambda bg=bg, sg_=sg_: nc.scalar.activation(out=sg_.t[:], in_=bg.t[:], func=AF.Silu),
                                  reads=[bg], writes=[sg_])
                            kb.op(dve, lambda bu=bu, sg_=sg_, jj=jp + c, at=at: nc.vector.tensor_tensor(
                                out=at.t[:, jj, :], in0=bu.t[:], in1=sg_.t[:], op=ALU.mult),
                                reads=[bu, sg_], writes=[at])

                    def down(pi):
                        j0, nj = FF_PARTS[pi]
                        at = actT[pi % 2]
                        for q in range(4):
                            slot = wslot()
                            wload(slot, "down", l, DFF, j0 * 128, q * 512, 512, 0, nk=nj)
                            for s in range(4):
                                bank = next_bank()

                                def mk(s=s, bank=bank, slot=slot, at=at, nj=nj):
                                    ins = None
                                    for jj in range(nj):
                                        ins = nc.tensor.matmul(bank.t[:], at.t[:, jj, s * 128:(s + 1) * 128],
                                                               slot.t[:, jj, :], start=(jj == 0), stop=(jj == nj - 1))
                                    return ins
                                kb.op(pe, mk, reads=[slot, at], writes=[bank])
                                xadd(bank, s, q)

                    npart = len(FF_PARTS)
                    for jp in range(0, FF_PARTS[0][1], 2):
                        gu_pair(0, jp)
                    for pi in range(npart):
                        if pi >= 1 and pi - 1 < len(ffn_events):
                            ffn_events[pi - 1]()
                        if pi % 2 == 0:
                            emit_casts(1)
                        if pi + 1 < npart:
                            gu_pair(pi + 1, 0)
                        down(pi)
                        if pi + 1 < npart:
                            for jp in range(2, FF_PARTS[pi + 1][1], 2):
                                gu_pair(pi + 1, jp)

                fin_cnt = [0]
                fin_ring = []
                if do_c:
                    fin_ring = [(b_, b_.t[:, 0:1024]) for b_ in (ost + onab + udb)]

                def final_tile(sg, t0):
                    for s in range(4):
                        hp = actT[0]
                        st = stats[3][s]
                        rms_stats(xt.t[:, s, :], hp, hp.t[:, 0:4, :].rearrange("p a b -> p (a b)"), [xts[s]], st, 0, D)
                        for hf in range(2):
                            o, ov = fin_ring[fin_cnt[0] % len(fin_ring)]
                            fin_cnt[0] += 1
                            kb.op(dve, lambda s=s, hf=hf, o=o, st=st: nc.vector.scalar_tensor_tensor(
                                out=ov, in0=xt.t[:, s, hf * 1024:(hf + 1) * 1024], scalar=st.t[:, 1:2],
                                in1=gA.t[:, hf * 1024:(hf + 1) * 1024], op0=ALU.mult, op1=ALU.mult),
                                reads=[xts[s], st, gA], writes=[o])
                            if sg["name"] == "S":
                                dst = ys.ap()[t0 + s * 128:t0 + (s + 1) * 128, hf * 1024:(hf + 1) * 1024]
                            else:
                                r0 = t0 - 2048 + s * 128
                                dst = yp.ap()[r0:r0 + 128, hf * 1024:(hf + 1) * 1024]
                            kb.dma(pool, dst, ov, o, reads=[o])

                xpref = [False]
                tiles = []
                for s in ("S", "P"):
                    sg = SEG[s]
                    lo, hi = RANGE_L[layer_prev][s] if do_c else RANGE_A0[s]
                    for t0 in range(lo, hi, T):
                        kvo = False
                        if s == "P" and do_a:
                            qlo, qhi = RANGE_L[a_layer][s]
                            kvo = not (qlo <= t0 < qhi)
                        tiles.append((sg, t0, kvo))
                if do_c:
                    for ev in mix_events(tiles[0]):
                        ev()
                for ti, (sg, t0, kvo) in enumerate(tiles):
                    nxt = tiles[ti + 1] if ti + 1 < len(tiles) else None
                    if do_c:
                        c_tile(layer_prev, sg, t0, ffn_events=(mix_events(nxt) if (last and nxt is not None) else ()))
                        if last:
                            final_tile(sg, t0)
                        else:
                            x_store(sg["X1"], t0)
                    else:
                        if not xpref[0]:
                            x_load(sg["xin"], t0)
                        xpref[0] = False
                    if do_a:
                        hTa = hTs[1] if do_c else hTs[ti % 2]
                        if do_c or ti == 0:
                            norm_T(gA, hTa, stats[0])
                        evs = mix_events(nxt) if (do_c and nxt is not None) else []
                        if nxt is not None:
                            if do_c:
                                nsrc = nxt[0]["xin"] if layer_prev == 0 else nxt[0]["X1"]
                            else:
                                nsrc = nxt[0]["xin"]

                            def xev(nsrc=nsrc, nxt=nxt):
                                x_load(nsrc, nxt[1])
                                xpref[0] = True
                            evs = [xev] + evs
                            if not do_c:
                                evs = evs + norm_events(gA, hTs[(ti + 1) % 2], stats[0])
                        qkv(a_layer, sg, t0, hTa, kv_only=kvo, events=evs, cast_spread=(not do_c))
                        if (not do_c) and ti == 0:
                            emit_zero_fill()
                kb.barrier()
            kb.retire(mark)

        def attn_phase(l):
            mark = len(kb.bufs)
            with contextlib.ExitStack() as esa:
                tp_banks[:] = []
                ps_banks[:] = [kb.ps("psa%d" % i, [128, 512], F32, esa) for i in range(8)]
                qts = [kb.sb("qts%d" % i, [128, 2, 2048], BF16, esa) for i in range(2)]
                kts = [kb.sb("kts%d" % i, [128, 2, 4096], BF16, esa) for i in range(2)]
                vbuf = [kb.sb("vbuf%d" % i, [128, 32, 2 * VW], BF16, esa) for i in range(2)]
                nsm = max(nsl.values())
                nab = [kb.sb("nab%d" % i, [128, nsm, 2, 128], BF16, esa) for i in range(2)]
                dlb = [kb.sb("dlbt%d" % i, [128, 3, 2, 2, 128], BF16, esa) for i in range(2)]
                pts = [kb.sb("pt%d" % i, [128, 256], BF16, esa) for i in range(4)]
                osts = [kb.sb("osta%d" % i, [128, 2, HD], F32, esa) for i in range(3)]
                usts = [kb.sb("usta%d" % i, [128, 2, VW], F32, esa) for i in range(3)]
                rls = [kb.sb("rls%d" % i, [128, 2], F32, esa) for i in range(4)]
                for u in usts:
                    kb.op(dve, lambda u=u: nc.vector.memset(u.t[:], 0.0), writes=[u])
                sc_slots = [(ps_banks[i], 0) for i in range(4)]
                u_banks = ps_banks[4:8]
                cnt = dict(sc=0, pt=0, ub=0, job=0, st=0, v=0, unit=0, rl=0)
                pending = []
                LAG = 3

                def flush(n_keep):
                    while len(pending) > n_keep:
                        pending.pop(0)()

                def tile_pair2(k_aps, q_aps, bias_ap, km_ap, km_buf, v_aps, ubs, first, lastt, rbufs, vb, fin):
                    scb, c0 = sc_slots[cnt["sc"] % 4]
                    cnt["sc"] += 1
                    ptb = pts[cnt["pt"] % 4]
                    cnt["pt"] += 1

                    def mk():
                        ins = None
                        for hh in range(2):
                            nc.tensor.matmul(scb.t[:, c0 + hh * 128:c0 + (hh + 1) * 128], k_aps[hh], q_aps[hh],
                                             start=True, stop=False)
                            ins = nc.tensor.matmul(scb.t[:, c0 + hh * 128:c0 + (hh + 1) * 128], ident.t[:],
                                                   bias_ap[:, hh * 128:(hh + 1) * 128], start=False, stop=True)
                        return ins
                    kb.op(pe, mk, reads=rbufs + [ident], writes=[scb])

                    def stage2():
                        if km_ap is None:
                            kb.op(act, lambda: nc.scalar.activation(out=ptb.t[:], in_=scb.t[:, c0:c0 + 256], func=AF.Exp),
                                  reads=[scb], writes=[ptb])
                        else:
                            kb.op(act, lambda: nc.scalar.activation(out=ptb.t[:], in_=scb.t[:, c0:c0 + 256], func=AF.Exp,
                                                                    bias=km_ap, scale=1.0),
                                  reads=[scb, km_buf], writes=[ptb])
                        for hh in range(2):
                            kb.op(pe, lambda hh=hh: nc.tensor.matmul(ubs[hh].t[:, 0:HD + 1], ptb.t[:, hh * 128:(hh + 1) * 128],
                                                                     v_aps[hh], start=first, stop=lastt),
                                  reads=[ptb, vb], writes=[ubs[hh]])
                        if lastt:
                            fin()
                    pending.append(stage2)
                    flush(LAG)

                def na_fin(ubs, ost, sg, tk, h0):
                    def fin():
                        for hh in range(2):
                            ub = ubs[hh]
                            rl = rls[cnt["rl"] % 4]
                            cnt["rl"] += 1
                            kb.op(dve, lambda: nc.vector.tensor_scalar_add(out=rl.t[:, 0:1], in0=ub.t[:, HD:HD + 1], scalar1=1e-30),
                                  reads=[ub], writes=[rl])
                            kb.op(dve, lambda: nc.vector.reciprocal(out=rl.t[:, 1:2], in_=rl.t[:, 0:1]), writes=[rl])
                            kb.op(dve, lambda: nc.vector.tensor_scalar_mul(out=ost.t[:, hh, :], in0=ub.t[:, 0:HD], scalar1=rl.t[:, 1:2]),
                                  reads=[ub, rl], writes=[ost])
                        kb.dma(pool, sg["ONA"].ap()[tk:tk + 128, h0 * HD:(h0 + 2) * HD],
                               ost.t[:].rearrange("p h d -> p (h d)"), ost, reads=[ost])
                    return fin

                def dil_fin(ubs, ust, sg, bi, tb, dil, hd):
                    def fin():
                        for hh in range(2):
                            ub = ubs[hh]
                            eng = act if hh == 0 else dve
                            copy_on(eng, ust.t[:, hh, 0:HD + 1], ub.t[:, 0:HD + 1], [ub], [ust])
                        dst = sg["UD"][bi].ap()[tb:tb + 127 * dil + 1:dil, hd * 2 * VW:(hd + 1) * 2 * VW]
                        kb.dma(pool, dst, ust.t[:].rearrange("p h c -> p (h c)"), ust, reads=[ust])
                    return fin

                for s in ("S", "P"):
                    sg = SEG[s]
                    lo, hi = RANGE_L[l][s]
                    plan, _ = _NA_PLANS[s]
                    kbase, _ = kml[s]
                    for hg in range(8):
                        h0 = 2 * hg
                        ui = cnt["unit"]
                        cnt["unit"] += 1

                        def table_load(s_, hg_, ui_):
                            if hg_ < 4:
                                nb_ = nab[ui_ % 2]
                                kb.dma(pool, nb_.t[:, 0:nsl[s_], :, :].rearrange("p a b c -> p (a b c)"),
                                       nab_d[s_].ap()[l, hg_], nb_, writes=[nb_])
                            else:
                                db_ = dlb[ui_ % 2]
                                kb.dma(pool, db_.t[:].rearrange("p a b c d -> p (a b c d)"), dlb_d.ap()[hg_ - 4], db_,
                                       writes=[db_])
                        if ui == 0:
                            table_load(s, hg, ui)
                        nxt_unit = (s, hg + 1) if hg < 7 else (("P", 0) if s == "S" else None)
                        if nxt_unit is not None:
                            table_load(nxt_unit[0], nxt_unit[1], ui + 1)
                        if hg < 4:
                            nb = nab[ui % 2]
                        else:
                            db = dlb[ui % 2]
                        for q0 in range(lo, hi, 2048):
                            qt = qts[cnt["st"] % 2]
                            kt = kts[cnt["st"] % 2]
                            cnt["st"] += 1
                            kb.dma(sp, qt.t[:], sg["QT"].ap()[h0:h0 + 2, :, PAD + q0:PAD + q0 + 2048].rearrange("h d t -> d h t"),
                                   qt, writes=[qt])
                            kb.dma(sp, kt.t[:], sg["KT"].ap()[h0:h0 + 2, :, q0:q0 + 4096].rearrange("h d t -> d h t"),
                                   kt, writes=[kt])
                            if hg < 4:
                                vb = vbuf[cnt["v"] % 2]
                                cnt["v"] += 1
                                r0 = PAD + q0 - 512
                                kb.dma(sp, vb.t[:, 0:24, :],
                                       sg["V"].ap()[r0:r0 + 24 * 128, h0 * VW:(h0 + 2) * VW].rearrange("(n p) c -> p n c", p=128),
                                       vb, writes=[vb])
                                for b in range(16):
                                    lq0 = q0 // GRID_W + 2 * b
                                    pairs = plan[lq0]
                                    ost = osts[cnt["job"] % 3]
                                    cnt["job"] += 1
                                    ubs = [u_banks[(cnt["ub"] + hh) % 4] for hh in range(2)]
                                    cnt["ub"] += 2
                                    fin = na_fin(ubs, ost, sg, q0 + 128 * b, h0)
                                    for pi, (ktr, slot) in enumerate(pairs):
                                        ktok = ktr * GRID_W
                                        koff = ktok - (q0 - 1024)
                                        vn = (ktok - (q0 - 512)) // 128
                                        assert 0 <= vn < 24 and 0 <= koff <= 4096 - 128
                                        tile_pair2([kt.t[:, hh, koff:koff + 128] for hh in range(2)],
                                                   [qt.t[:, hh, 128 * b:128 * b + 128] for hh in range(2)],
                                                   nb.t[:, slot, :, :].rearrange("p h q -> p (h q)"), None, None,
                                                   [vb.t[:, vn, hh * VW:hh * VW + HD + 1] for hh in range(2)],
                                                   ubs, pi == 0, pi == len(pairs) - 1, [kt, qt, nb], vb, fin)
                            else:
                                hd = hg - 4
                                for bi, dil in enumerate(DILS):
                                    nblk = 2048 // (128 * dil)
                                    ntile = nblk + 1
                                    vb = vbuf[cnt["v"] % 2]
                                    cnt["v"] += 1
                                    mA = q0 // dil - 64
                                    for r in range(dil):
                                        base = PAD + mA * dil + r
                                        src = sg["V"].ap()[base:base + dil * (128 * ntile - 1) + 1:dil, h0 * VW:(h0 + 2) * VW]
                                        kb.dma(sp, vb.t[:, r * ntile:(r + 1) * ntile, :],
                                               src.rearrange("(n p) c -> p n c", p=128), vb, writes=[vb])
                                    kb0, kcnt = kbase[bi]
                                    for r in range(dil):
                                        for blk in range(nblk):
                                            ust = usts[cnt["job"] % 3]
                                            cnt["job"] += 1
                                            qc0 = blk * 128 * dil + r
                                            ubs = [u_banks[(cnt["ub"] + hh) % 4] for hh in range(2)]
                                            cnt["ub"] += 2
                                            fin = dil_fin(ubs, ust, sg, bi, q0 + blk * 128 * dil + r, dil, hd)
                                            for ktile in range(2):
                                                kc0 = (-64 + 128 * (blk + ktile)) * dil + r + 1024
                                                m_t = mA + 128 * (blk + ktile)
                                                assert (m_t + 64) % 64 == 0 and 0 <= (m_t + 64) // 64 < kcnt
                                                kmcol = kb0 + r * kcnt + (m_t + 64) // 64
                                                assert 0 <= kc0 and kc0 + 127 * dil < 4096
                                                tile_pair2([kt.t[:, hh, kc0:kc0 + 127 * dil + 1:dil] for hh in range(2)],
                                                           [qt.t[:, hh, qc0:qc0 + 127 * dil + 1:dil] for hh in range(2)],
                                                           db.t[:, bi, ktile, :, :].rearrange("p h q -> p (h q)"),
                                                           kmt[s].t[:, kmcol:kmcol + 1], kmt[s],
                                                           [vb.t[:, r * ntile + blk + ktile, hh * VW:hh * VW + HD + 1] for hh in range(2)],
                                                           ubs, ktile == 0, ktile == 1, [kt, qt, db], vb, fin)
                    flush(0)
                flush(0)
                kb.barrier()
            kb.retire(mark)

        dense_phase(None, False, True, 0, False)
        emit_casts(None, layer=0)
        if DEBUG_STOP != "A0":
            attn_phase(0)
            if DEBUG_STOP != "B0":
                dense_phase(0, True, True, 1, False)
                if DEBUG_STOP != "C0":
                    attn_phase(1)
                    dense_phase(1, True, False, None, True)
        kb.barrier()
    return nc


_NC_CACHE = {}


def kernel(x_prompt, x_sample, w_in, w_out, g_attn, g_na, g_dil, rpb_na, t5_table, g_ffn, w_gate, w_up, w_down,
           g_final):
    f = lambda a: np.ascontiguousarray(np.asarray(a, dtype=np.float32))
    x_prompt, x_sample = f(x_prompt), f(x_sample)
    w_in, w_out, w_gate, w_up, w_down = f(w_in), f(w_out), f(w_gate), f(w_up), f(w_down)
    g_attn, g_ffn, g_final = f(g_attn), f(g_ffn), f(g_final)
    rpb_na, t5_table = f(rpb_na), f(t5_table)
    g_mix = np.ascontiguousarray(np.concatenate([f(g_na), f(g_dil)], axis=1))
    if "nc" not in _NC_CACHE:
        _NC_CACHE["nc"] = build_program()
    nc = _NC_CACHE["nc"]
    ident = np.eye(128, dtype=np.float32)
    dlb = np.stack([_dil_table(t5_table, hd).reshape(128, -1) for hd in range(4)])
    xpad = np.zeros((2048 + 16384 + 2048, D), np.float32)
    xpad[2048:2048 + 16384] = x_prompt[0]
    nabS = np.stack([np.stack([_na_table("S", 0, rpb_na[l], hg).reshape(128, -1) for hg in range(4)])
                     for l in range(DEPTH)])
    kmS = _km_table("S", 0)
    in_maps = []
    for c in range(NCORES):
        nabP = np.stack([np.stack([_na_table("P", c, rpb_na[l], hg).reshape(128, -1) for hg in range(4)])
                         for l in range(DEPTH)])
        in_maps.append({
            "xs": x_sample[c], "xp": np.ascontiguousarray(xpad[2048 * c:2048 * c + LP]),
            "w_in": w_in, "w_out": w_out, "w_gate": w_gate, "w_up": w_up, "w_down": w_down,
            "g_attn": g_attn, "g_ffn": g_ffn, "g_mix": g_mix, "g_final": g_final.reshape(1, D),
            "identd": ident, "zerosd": np.zeros((128, NH * VW), np.float32), "nabS": nabS, "nabP": nabP, "dlb": dlb, "kmS": kmS, "kmP": _km_table("P", c),
        })
    res = run_bass_kernel_spmd(nc, in_maps, core_ids=list(range(NCORES)))
    y_sample = np.stack([np.asarray(res.results[c]["ys"], dtype=np.float32) for c in range(NCORES)])
    y_prompt = np.concatenate([np.asarray(res.results[c]["yp"], dtype=np.float32) for c in range(NCORES)])[None]
    return (y_prompt, y_sample)
```
